# Optimizing a Trainium2 kernel written in Bass

```python
import math
import jax, jax.numpy as jnp
from jax import lax
import numpy as np

D_MODEL = 1024
BATCH = 4
SEQ = 8192
DEPTH = 1

PLE_DIM = 256
CHUNK = 128
A_GROUPS = 8
A_GROUP_DIM = D_MODEL // A_GROUPS
A_WIDTH = A_GROUPS * A_GROUP_DIM
B_HEADS = 8
B_QK_DIM = 64
B_V_DIM = 2 * B_QK_DIM
B_QK_WIDTH = B_HEADS * 2 * B_QK_DIM
B_WIDTH = B_HEADS * B_V_DIM
Q_BLOCK = 128
ROPE_THETA = 10000.0
LN_EPS = 1e-5
RMS_EPS = 1e-5
DEEPNORM_ALPHA = (2 * DEPTH) ** 0.25
DEEPNORM_BETA = (8 * DEPTH) ** -0.25
IN_SIZES = (A_WIDTH, A_WIDTH, A_WIDTH, B_QK_WIDTH, B_QK_WIDTH, B_WIDTH, B_WIDTH, D_MODEL, D_MODEL)
IN_WIDTH = sum(IN_SIZES)
IN_OFFSETS = tuple(int(o) for o in np.cumsum(IN_SIZES)[:-1])
B_V_OFFSET = 3 * A_WIDTH + 2 * B_QK_WIDTH

kernel_name = "hybrid_gmlp_diffattn_gated_deepnorm"


def layer_norm(x, g, b):
    xf = x.astype(jnp.float32)
    mu = jnp.mean(xf, axis=-1, keepdims=True)
    xc = xf - mu
    var = jnp.mean(xc * xc, axis=-1, keepdims=True)
    y = xc * lax.rsqrt(var + LN_EPS) * g.astype(jnp.float32) + b.astype(jnp.float32)
    return y.astype(x.dtype)


def rms_norm(x, g):
    xf = x.astype(jnp.float32)
    y = xf * lax.rsqrt(jnp.mean(xf * xf, axis=-1, keepdims=True) + RMS_EPS) * g.astype(jnp.float32)
    return y.astype(x.dtype)


def rope_tables(positions, dim):
    inv_freq = ROPE_THETA ** (-jnp.arange(0, dim, 2, dtype=jnp.float32) / dim)
    ang = positions.astype(jnp.float32)[..., None] * inv_freq
    return jnp.cos(ang), jnp.sin(ang)


def apply_rope(t, cos, sin):
    c = cos[:, :, None, None, :].astype(t.dtype)
    s = sin[:, :, None, None, :].astype(t.dtype)
    t1, t2 = jnp.split(t, 2, axis=-1)
    return jnp.concatenate([t1 * c - t2 * s, t2 * c + t1 * s], axis=-1)


def diff_attention(q, k, v, lam):
    bn, s_len = q.shape[0], q.shape[1]
    nb = s_len // Q_BLOCK
    qb = q.reshape(bn, nb, Q_BLOCK, B_HEADS, 2, B_QK_DIM).transpose(1, 0, 2, 3, 4, 5)
    starts = jnp.arange(nb, dtype=jnp.int32) * Q_BLOCK
    kpos = jnp.arange(s_len, dtype=jnp.int32)

    def block(args):
        qblk, start = args
        qpos = start + jnp.arange(Q_BLOCK, dtype=jnp.int32)
        mask = kpos[None, :] <= qpos[:, None]
        sc = jnp.einsum("bqhmd,bkhmd->bhmqk", qblk, k,
                        preferred_element_type=jnp.float32)
        sc = jnp.where(mask, sc, -jnp.inf)
        pr = jax.nn.softmax(sc, axis=-1)
        a = pr[:, :, 0] - lam * pr[:, :, 1]
        return jnp.einsum("bhqk,bkhd->bqhd", a.astype(v.dtype), v)

    o = lax.map(block, (qb, starts))
    return o.transpose(1, 0, 2, 3, 4).reshape(bn, s_len, B_HEADS, B_V_DIM)


def setup_inputs(seed: int = 0) -> dict:
    key = jax.random.key(seed)
    ks = jax.random.split(key, 24)
    f32 = jnp.float32
    nrm = lambda k, shape, scale: jax.random.normal(k, shape, f32) * scale
    x = nrm(ks[0], (BATCH, SEQ, D_MODEL), 1.0)
    p = nrm(ks[1], (DEPTH, BATCH, SEQ, PLE_DIM), 1.0)
    offset = jax.random.randint(ks[2], (BATCH, 1), 0, 1024, dtype=jnp.int32)
    positions = offset + jnp.arange(SEQ, dtype=jnp.int32)[None, :]
    w_in = nrm(ks[3], (DEPTH, D_MODEL, IN_WIDTH), D_MODEL ** -0.5)
    w_in = w_in.at[:, :, B_V_OFFSET:B_V_OFFSET + B_WIDTH].multiply(DEEPNORM_BETA)
    a_ln_g = 1.0 + nrm(ks[4], (DEPTH, A_WIDTH), 0.05)
    a_ln_b = nrm(ks[5], (DEPTH, A_WIDTH), 0.02)
    a_w_s = nrm(ks[6], (DEPTH, A_GROUPS, CHUNK, CHUNK), CHUNK ** -0.5)
    a_b_s = 1.0 + nrm(ks[7], (DEPTH, A_GROUPS, CHUNK), 0.1)
    b_lam_q1 = nrm(ks[8], (DEPTH, B_QK_DIM), 0.1)
    b_lam_k1 = nrm(ks[9], (DEPTH, B_QK_DIM), 0.1)
    b_lam_q2 = nrm(ks[10], (DEPTH, B_QK_DIM), 0.1)
    b_lam_k2 = nrm(ks[11], (DEPTH, B_QK_DIM), 0.1)
    b_subln_g = 1.0 + nrm(ks[12], (DEPTH, B_V_DIM), 0.05)
    w_branch_a = nrm(ks[13], (DEPTH, A_WIDTH, D_MODEL), A_WIDTH ** -0.5 * DEEPNORM_BETA)
    w_branch_b = nrm(ks[14], (DEPTH, B_WIDTH, D_MODEL), B_WIDTH ** -0.5 * DEEPNORM_BETA)
    w_out = nrm(ks[15], (DEPTH, D_MODEL, D_MODEL), D_MODEL ** -0.5 * DEEPNORM_BETA)
    w_ple = nrm(ks[16], (DEPTH, PLE_DIM, D_MODEL), PLE_DIM ** -0.5 * DEEPNORM_BETA)
    w_ple_gate = nrm(ks[17], (DEPTH, D_MODEL, D_MODEL), D_MODEL ** -0.5)
    ln_g = 1.0 + nrm(ks[18], (DEPTH, D_MODEL), 0.05)
    ln_b = nrm(ks[19], (DEPTH, D_MODEL), 0.02)
    return {"x": x, "p": p, "positions": positions, "w_in": w_in,
            "a_ln_g": a_ln_g, "a_ln_b": a_ln_b, "a_w_s": a_w_s, "a_b_s": a_b_s,
            "b_lam_q1": b_lam_q1, "b_lam_k1": b_lam_k1, "b_lam_q2": b_lam_q2, "b_lam_k2": b_lam_k2,
            "b_subln_g": b_subln_g, "w_branch_a": w_branch_a, "w_branch_b": w_branch_b,
            "w_out": w_out, "w_ple": w_ple, "w_ple_gate": w_ple_gate,
            "ln_g": ln_g, "ln_b": ln_b}


def reference(x, p, positions, w_in, a_ln_g, a_ln_b, a_w_s, a_b_s,
              b_lam_q1, b_lam_k1, b_lam_q2, b_lam_k2, b_subln_g,
              w_branch_a, w_branch_b, w_out, w_ple, w_ple_gate, ln_g, ln_b):
    bn, s_len, _ = x.shape
    n_chunks = s_len // CHUNK
    cos, sin = rope_tables(positions, B_QK_DIM)
    causal_chunk = jnp.tril(jnp.ones((CHUNK, CHUNK), dtype=bool))
    for i in range(DEPTH):
        lam_init = 0.8 - 0.6 * math.exp(-0.3 * i)
        h = x @ w_in[i]
        ua, va, za, qh, kh, vh, zb, ga, gb = jnp.split(h, IN_OFFSETS, axis=-1)

        ua = jax.nn.gelu(ua, approximate=False)
        va = layer_norm(jax.nn.gelu(va, approximate=False), a_ln_g[i], a_ln_b[i])
        vc = va.reshape(bn, n_chunks, CHUNK, A_GROUPS, A_GROUP_DIM)
        ws = jnp.where(causal_chunk, a_w_s[i], 0.0)
        sa = jnp.einsum("gts,bnsgc->bntgc", ws, vc) + a_b_s[i].T[:, :, None]
        ya = ua * sa.reshape(bn, s_len, A_WIDTH) * jax.nn.silu(za)

        q = apply_rope(qh.reshape(bn, s_len, B_HEADS, 2, B_QK_DIM), cos, sin) * (B_QK_DIM ** -0.5)
        k = apply_rope(kh.reshape(bn, s_len, B_HEADS, 2, B_QK_DIM), cos, sin)
        lam = (jnp.exp(jnp.sum(b_lam_q1[i].astype(jnp.float32) * b_lam_k1[i].astype(jnp.float32)))
               - jnp.exp(jnp.sum(b_lam_q2[i].astype(jnp.float32) * b_lam_k2[i].astype(jnp.float32)))
               + lam_init)
        o = diff_attention(q, k, vh.reshape(bn, s_len, B_HEADS, B_V_DIM), lam)
        o = rms_norm(o, b_subln_g[i]) * (1.0 - lam_init)
        yb = o.reshape(bn, s_len, B_WIDTH) * jax.nn.silu(zb)

        merged = jax.nn.sigmoid(ga) * (ya @ w_branch_a[i]) + jax.nn.sigmoid(gb) * (yb @ w_branch_b[i])
        mix_out = merged @ w_out[i]

        ple = jax.nn.sigmoid(x @ w_ple_gate[i]) * (p[i] @ w_ple[i])

        x = layer_norm(DEEPNORM_ALPHA * x + mix_out + ple, ln_g[i], ln_b[i])
    return x
```

```python
import bisect
import contextlib
import math

import numpy as np
import concourse.bass as bass
import concourse.mybir as mybir
from concourse.bass_utils import run_bass_kernel_spmd

F32 = mybir.dt.float32
BF16 = mybir.dt.bfloat16
I32 = mybir.dt.int32
AF = mybir.ActivationFunctionType
ALU = mybir.AluOpType

NCORES = 8
D = 1024
SEQ = 8192
NT = 4096
TS = 512
NTILE = NT // TS
LN_EPS = 1e-5
RMS_EPS = 1e-5
ALPHA = 2.0 ** 0.25
LAM_INIT = 0.8 - 0.6 * math.exp(0.0)
NEG = -30000.0


class Buf:
    __slots__ = ("name", "w", "r", "rd")

    def __init__(self, name):
        self.name = name
        self.w = None
        self.r = {}
        self.rd = []


class Op:
    __slots__ = ("eng", "fn", "seq", "deps", "inc", "key", "cnt")


class Prog:
    ENG = ("pe", "act", "dve", "pool", "sp")

    def __init__(self):
        self.ops = []
        self.by_eng = {e: [] for e in self.ENG}
        self.key_seqs = {}
        self.key_last = {}
        self.pending = {e: [] for e in self.ENG}

    def op(self, eng, fn, reads=(), writes=(), key=None):
        o = Op()
        o.eng, o.fn, o.seq, o.inc, o.key, o.cnt = eng, fn, len(self.ops), False, key, 0
        deps = []
        for b in reads:
            if b.w is not None:
                deps.append(b.w)
        for b in writes:
            if b.w is not None:
                deps.append(b.w)
            deps.extend(b.r.values())
            deps.extend(b.rd)
        if self.pending[eng]:
            deps.extend(self.pending[eng])
            self.pending[eng] = []
        for b in reads:
            if key is None:
                b.r[eng] = o
            else:
                b.rd.append(o)
        for b in writes:
            b.w = o
            b.r = {}
            b.rd = []
        o.deps = [d for d in dict.fromkeys(deps)
                  if d is not o and not (d.eng == "pe" and eng == "pe" and d.key is None and key is None)]
        for d in o.deps:
            d.inc = True
        self.ops.append(o)
        self.by_eng[eng].append(o)
        if key is not None:
            self.key_seqs.setdefault(key, []).append(o.seq)
            self.key_last[key] = o
        return o

    def barrier(self):
        last = []
        for e in self.ENG:
            for o in reversed(self.by_eng[e]):
                if o.key is None:
                    last.append(o)
                    break
        last.extend(self.key_last.values())
        for o in last:
            o.inc = True
        for e in self.ENG:
            self.pending[e] = list(last)


    def mm(self, out, lhsT, rhs, start, stop, reads, writes, tp=None):
        kw = {} if tp is None else {"tile_position": tp}
        return self.op("pe", lambda e: e.matmul(out, lhsT=lhsT, rhs=rhs, start=start, stop=stop, **kw), reads, writes)

    def act(self, out, in_, func, reads, writes, bias=None, scale=None):
        kw = {}
        if bias is not None:
            kw["bias"] = bias
        if scale is not None:
            kw["scale"] = scale
        return self.op("act", lambda e: e.activation(out=out, in_=in_, func=func, **kw), reads, writes)

    def copy(self, eng, out, in_, reads, writes):
        if eng == "act":
            fn = lambda e: e.copy(out=out, in_=in_)
        else:
            fn = lambda e: e.tensor_copy(out=out, in_=in_)
        return self.op(eng, fn, reads, writes)

    def tt(self, eng, out, in0, in1, op, reads, writes):
        return self.op(eng, lambda e: e.tensor_tensor(out=out, in0=in0, in1=in1, op=op), reads, writes)

    def ts(self, eng, out, in0, s1, s2, op0, op1, reads, writes):
        if op1 is None:
            fn = lambda e: e.tensor_scalar(out=out, in0=in0, scalar1=s1, scalar2=None, op0=op0)
        else:
            fn = lambda e: e.tensor_scalar(out=out, in0=in0, scalar1=s1, scalar2=s2, op0=op0, op1=op1)
        return self.op(eng, fn, reads, writes)

    def stt(self, eng, out, in0, scalar, in1, op0, op1, reads, writes):
        return self.op(eng, lambda e: e.scalar_tensor_tensor(out=out, in0=in0, scalar=scalar, in1=in1, op0=op0, op1=op1), reads, writes)

    def dma(self, q, out, in_, reads, writes, key):
        return self.op(q, lambda e: e.dma_start(out=out, in_=in_), reads, writes, key=key)

    def memset(self, eng, out, val, reads, writes):
        return self.op(eng, lambda e: e.memset(out, val), reads, writes)

    def emit(self, nc, final_waits=()):
        keys = list(self.key_seqs)
        with contextlib.ExitStack() as st:
            esem = {e: st.enter_context(nc.semaphore("s_" + e)) for e in self.ENG}
            ksem = {k: st.enter_context(nc.semaphore("k_%s" % (k,))) for k in keys}
            for e in self.ENG:
                c = 0
                for o in self.by_eng[e]:
                    if o.key is None and o.inc:
                        c += 1
                        o.cnt = c
            block = st.enter_context(nc.Block())
            engobj = {"pe": "tensor", "act": "scalar", "dve": "vector", "pool": "gpsimd", "sp": "sync"}

            def run(e, eng):
                waited = {}

                def need(d, seq):
                    if d.key is None:
                        return ("e", d.eng), esem[d.eng], d.cnt
                    return ("k", d.key), ksem[d.key], 16 * bisect.bisect_left(self.key_seqs[d.key], seq)

                def dowait(k, s, v):
                    if waited.get(k, 0) < v:
                        waited[k] = v
                        eng.wait_ge(s, v)

                for o in self.by_eng[e]:
                    for d in o.deps:
                        dowait(*need(d, o.seq))
                    ins = o.fn(eng)
                    if o.key is not None:
                        ins.then_inc(ksem[o.key], 16)
                    elif o.inc:
                        ins.then_inc(esem[e], 1)
                if e == "sp":
                    for d in final_waits:
                        dowait(*need(d, len(self.ops)))

            for e in self.ENG:
                getattr(block, engobj[e])(lambda eng, e=e: run(e, eng))


IN_OFF = {"ua": 0, "va": 1024, "za": 2048, "q": 3072, "k": 4096, "v": 5120, "zb": 6144, "ga": 7168, "gb": 8192}
CH = ["va", "ua", "za", "wa", "ga", "zb", "wb", "gb", "wout", "wpg", "wp"]


def build_program():
    nc = bass.Bass("TRN2", target_bir_lowering=False)
    dram_in = lambda n, s, d=F32: nc.dram_tensor(n, list(s), d, kind="ExternalInput").ap()
    xT_all = dram_in("xT_all", [D, SEQ])
    x_own = dram_in("x_own", [NT, D])
    pT_d = dram_in("pT", [256, NT])
    pos_d = dram_in("pos", [128, 64], I32)
    w_in = dram_in("w_in", [D, 9216])
    w_a = dram_in("w_a", [D, D])
    w_b = dram_in("w_b", [D, D])
    w_out = dram_in("w_out", [D, D])
    w_pg = dram_in("w_pg", [D, D])
    w_p = dram_in("w_p", [256, D])
    wsT_d = dram_in("wsT", [8, 128, 128])
    tri_d = dram_in("tri", [128, 128])
    bs_d = dram_in("bs", [8, 128])
    alng_d = dram_in("alng", [128, 8])
    alnb_d = dram_in("alnb", [128, 8])
    subg_d = dram_in("subg", [128, 1])
    lamv_d = dram_in("lamv", [4, 64])
    lng_d = dram_in("lng", [D])
    lnb_d = dram_in("lnb", [D])
    vis_d = dram_in("vis", [128, 8])
    invf_d = dram_in("invf", [128, 32])
    out_d = nc.dram_tensor("out", [NT, D], F32, kind="ExternalOutput").ap()

    scr = lambda n, s: nc.dram_tensor(n, list(s), BF16, kind="Internal").ap()
    KTs = scr("KTs", [8, 128, SEQ])
    Vs = scr("Vs", [128, 64, 8, 128])
    QTs = scr("QTs", [8, 128, NT])
    xTb = scr("xTb", [128, 8, NT])
    ATs = scr("ATs", [128, 8, NT])
    Wsc = {n: scr("W_" + n, [256 if n == "wp" else D, D]) for n in CH}
    wsrc = {"wa": w_a, "wb": w_b, "wout": w_out, "wpg": w_pg, "wp": w_p}
    for n in ("va", "ua", "za", "ga", "zb", "gb"):
        wsrc[n] = w_in[:, IN_OFF[n]:IN_OFF[n] + 1024]

    P = Prog()
    MUL, ADD, SUB = ALU.mult, ALU.add, ALU.subtract
    with contextlib.ExitStack() as st:
        ARENA_COLS = 53000
        arena = st.enter_context(nc.sbuf_tensor("arena", [128, ARENA_COLS], F32))
        abf = arena.bitcast(BF16)
        ai32 = arena.bitcast(I32)
        psum = st.enter_context(nc.psum_tensor("psum", [128, 4096], F32))
        off = [0]

        def alloc(cols, dt=F32):
            nb = cols * (2 if dt == BF16 else 4)
            nb = (nb + 63) // 64 * 64
            o = off[0]
            off[0] += nb
            assert off[0] <= ARENA_COLS * 4, "SBUF arena overflow %d" % off[0]
            if dt == BF16:
                return abf[:, o // 2:o // 2 + cols]
            if dt == I32:
                return ai32[:, o // 4:o // 4 + cols]
            return arena[:, o // 4:o // 4 + cols]

        Bps = [Buf("ps%d" % i) for i in range(8)]
        bank = lambda i: psum[:, i * 512:(i + 1) * 512]
        bank2 = lambda i: psum[:, i * 512:(i + 2) * 512]

        def rsqrt(eng, out, a, tmp, B, iters):
            oi, ai = out.bitcast(I32), a.bitcast(I32)
            P.ts(eng, oi, ai, 1, None, ALU.arith_shift_right, None, B, B)
            P.ts(eng, oi, oi, -1, 0x5f3759df, MUL, ADD, B, B)
            for _ in range(iters):
                P.stt(eng, tmp, out, -0.5, out, MUL, MUL, B, B)
                P.tt(eng, tmp, tmp, a, MUL, B, B)
                P.stt(eng, out, tmp, 1.5, out, ADD, MUL, B, B)

        Bc = Buf("consts")
        identf = alloc(128); ident = alloc(128, BF16)
        onesf = alloc(128); ones_bf = alloc(128, BF16)
        trif = alloc(128); tri_bf = alloc(128, BF16)
        sel1 = alloc(128); sel2 = alloc(128)
        vis = alloc(8); subg = alloc(1); gsub = alloc(1); neglam = alloc(1)
        lamt = alloc(256); lamp = alloc(128); lams = alloc(2)
        invf = alloc(32)
        alng = alloc(8); alnb = alloc(8)
        posi = alloc(64, I32); posf = alloc(64)
        for o_, i_ in ((trif, tri_d), (vis, vis_d), (subg, subg_d), (invf, invf_d), (alng, alng_d), (alnb, alnb_d), (posi, pos_d),
                       (lamt, lamv_d.rearrange("a b -> (a b)").partition_broadcast(128))):
            P.dma("sp", o_, i_, [], [Bc], "c")
        P.copy("dve", tri_bf, trif, [Bc], [Bc])
        negm = alloc(128, BF16)
        P.ts("dve", negm, trif, -1.0, -NEG, ADD, MUL, [Bc], [Bc])
        P.tt("dve", lamp[:, 0:64], lamt[:, 0:64], lamt[:, 64:128], MUL, [Bc], [Bc])
        P.tt("dve", lamp[:, 64:128], lamt[:, 128:192], lamt[:, 192:256], MUL, [Bc], [Bc])
        P.op("dve", lambda e: e.reduce_sum(out=lams, in_=lamp.rearrange("p (a b) -> p a b", a=2), axis=mybir.AxisListType.X), [Bc], [Bc])
        P.act(lams, lams, AF.Exp, [Bc], [Bc])
        P.tt("dve", neglam, lams[:, 1:2], lams[:, 0:1], SUB, [Bc], [Bc])
        P.ts("dve", neglam, neglam, -LAM_INIT, None, ADD, None, [Bc], [Bc])
        P.ts("dve", gsub, subg, 1.0 - LAM_INIT, None, MUL, None, [Bc], [Bc])
        base = off[0]

        BW = {n: Buf("W_" + n) for n in CH}

        wqkv = alloc(8 * 3072, BF16); Bwq = [Buf("wqkv%d" % i) for i in range(6)]
        wq3 = wqkv.rearrange("p (k c) -> p k c", k=8)
        xt = [alloc(8 * TS, BF16) for _ in range(2)]; Bxt = [Buf("xt0"), Buf("xt1")]
        cosT = alloc(2048); sinT = alloc(2048); nsinT = alloc(2048); Btab = Buf("tab")
        ropeA = [alloc(1024) for _ in range(2)]; ropeB = [alloc(1024) for _ in range(2)]
        BropeA = [Buf("ropeA0"), Buf("ropeA1")]; BropeB = [Buf("ropeB0"), Buf("ropeB1")]
        rot = [[alloc(1024, BF16) for _ in range(2)] for _ in range(2)]; Brot = [[Buf("rot%d%d" % (w, b)) for b in range(2)] for w in range(2)]
        qst = [alloc(8 * TS, BF16) for _ in range(2)]; Bqst = [Buf("qst0"), Buf("qst1")]
        kst = [alloc(8 * TS, BF16) for _ in range(2)]; Bkst = [Buf("kst0"), Buf("kst1")]
        vst = [alloc(4 * 1024, BF16) for _ in range(2)]; Bvst = [Buf("vst0"), Buf("vst1")]
        ang = alloc(2048); kf = alloc(2048); ki = alloc(2048, I32); msk = alloc(2048)

        xT_v = xT_all.rearrange("(k p) n -> p k n", p=128)

        def load_wq(gi):
            c0 = IN_OFF["q"] + gi * 512
            P.dma("pool", wq3[:, :, gi * 512:(gi + 1) * 512], w_in[:, c0:c0 + 512].rearrange("(k p) c -> p k c", p=128), [], [Bwq[gi]], "wqkv")

        def load_xt(t):
            s = t % 2
            P.dma("pool", xt[s].rearrange("p (k n) -> p k n", k=8), xT_v[:, :, t * TS:(t + 1) * TS], [], [Bxt[s]], "xt%d" % s)

        load_wq(0)
        load_xt(0)
        load_wq(1)
        Bcp = Buf("consts_pool")
        P.memset("pool", identf, 1.0, [], [Bcp])
        P.op("pool", lambda e: e.affine_select(out=identf, in_=identf, pattern=[[-1, 128]], compare_op=ALU.is_equal, fill=0.0, base=0, channel_multiplier=1), [Bcp], [Bcp])
        P.copy("pool", ident, identf, [Bcp], [Bcp])
        P.memset("pool", onesf, 1.0, [], [Bcp])
        P.memset("pool", ones_bf, 1.0, [], [Bcp])
        P.memset("pool", sel1, 0.0, [], [Bcp])
        P.memset("pool", sel2, 0.0, [], [Bcp])
        P.memset("pool", sel1[0:1, :], 1.0, [Bcp], [Bcp])
        P.memset("pool", sel1[64:65, :], 1.0, [Bcp], [Bcp])
        P.memset("pool", sel2[32:33, :], 1.0, [Bcp], [Bcp])
        P.memset("pool", sel2[96:97, :], 1.0, [Bcp], [Bcp])
        for gi in range(2, 6):
            load_wq(gi)
        load_xt(1)
        wconv_left = list(CH)

        def wconv_one():
            if wconv_left:
                n = wconv_left.pop(0)
                P.dma("pool", Wsc[n], wsrc[n], [], [BW[n]], "wconv")

        Bt_ = [Btab, Bc]
        P.copy("dve", posf, posi, [Bc], [Bc])
        P.tt("dve", ang.rearrange("p (b i) -> p b i", b=64), posf.unsqueeze(2).broadcast_to([128, 64, 32]), invf.unsqueeze(1).broadcast_to([128, 64, 32]), MUL, [Bc], [Btab])
        C1 = 6.28125
        C2 = 2 * math.pi - 6.28125

        def wrap(t):
            P.ts("dve", msk, t, math.pi, -2 * math.pi, ALU.is_gt, MUL, Bt_, Bt_)
            P.tt("dve", t, t, msk, ADD, Bt_, Bt_)
            P.ts("dve", msk, t, -math.pi, 2 * math.pi, ALU.is_lt, MUL, Bt_, Bt_)
            P.tt("dve", t, t, msk, ADD, Bt_, Bt_)

        P.ts("dve", kf, ang, 1.0 / (2 * math.pi), None, MUL, None, Bt_, Bt_)
        P.copy("dve", ki, kf, Bt_, Bt_)
        P.copy("dve", kf, ki, Bt_, Bt_)
        P.stt("dve", ang, kf, -C1, ang, MUL, ADD, Bt_, Bt_)
        P.stt("dve", ang, kf, -C2, ang, MUL, ADD, Bt_, Bt_)
        wrap(ang)
        P.act(sinT, ang, AF.Sin, Bt_, Bt_)
        P.ts("dve", ang, ang, math.pi / 2, None, ADD, None, Bt_, Bt_)
        wrap(ang)
        P.act(cosT, ang, AF.Sin, Bt_, Bt_)
        P.ts("dve", nsinT, sinT, -1.0, None, MUL, None, Bt_, Bt_)

        BKT = [Buf("KT%d" % t) for t in range(16)]
        BV = [Buf("V%d" % t) for t in range(16)]
        BQT = [Buf("QT%d" % t) for t in range(8)]
        BxTb = [Buf("xTb%d" % t) for t in range(8)]
        cos3 = cosT.rearrange("p (b i) -> p b i", b=64)
        sin3 = sinT.rearrange("p (b i) -> p b i", b=64)
        nsin3 = nsinT.rearrange("p (b i) -> p b i", b=64)
        gcount = [0]
        pendB = []

        def p1_store(t):
            s = t % 2
            if t < 8:
                P.dma("sp", xTb[:, :, t * TS:(t + 1) * TS], xt[s].rearrange("p (k n) -> p k n", k=8), [Bxt[s]], [BxTb[t]], "stx%d" % s)
                P.dma("sp", QTs.rearrange("h p n -> p h n")[:, :, t * TS:(t + 1) * TS], qst[s].rearrange("p (h n) -> p h n", h=8), [Bqst[s]], [BQT[t]], "stq%d" % s)
            P.dma("sp", KTs.rearrange("h p n -> p h n")[:, :, t * TS:(t + 1) * TS], kst[s].rearrange("p (h n) -> p h n", h=8), [Bkst[s]], [BKT[t]], "stk%d" % s)
            P.dma("sp", Vs[:, t * 4:(t + 1) * 4, :, :].rearrange("p b h d -> p b (h d)"), vst[s].rearrange("p (b c) -> p b c", b=4), [Bvst[s]], [BV[t]], "stv%d" % s)

        def p1_B(t, blk, which):
            s = t % 2
            stage, Bstage = (qst[s], Bqst[s]) if which == 0 else (kst[s], Bkst[s])
            st3 = stage.rearrange("p (h n) -> p h n", h=8)
            for half in range(2):
                bk = 6 + half
                for hh in range(4):
                    h = half * 4 + hh
                    P.mm(bank(bk)[:, hh * 128:(hh + 1) * 128], rot[which][blk % 2][:, h * 128:(h + 1) * 128], ident, True, True, [Brot[which][blk % 2], Bcp], [Bps[bk]])
                P.copy("act", st3[:, half * 4:(half + 1) * 4, blk * 128:(blk + 1) * 128], bank(bk).rearrange("p (h n) -> p h n", h=4), [Bps[bk]], [Bstage])
            if blk == 3 and which == 1:
                p1_store(t)
                if t + 2 < 16:
                    load_xt(t + 2)
                wconv_one()

        def p1_group(t, blk, typ):
            s = t % 2
            tb = t * 4 + blk
            x3 = xt[s].rearrange("p (k n) -> p k n", k=8)
            pair = (gcount[0] % 3) * 2
            gcount[0] += 1
            for half in range(2):
                c0 = typ * 1024 + half * 512
                for kc in range(8):
                    P.mm(bank(pair + half), x3[:, kc, blk * 128:(blk + 1) * 128], wq3[:, kc, c0:c0 + 512], kc == 0, kc == 7, [Bxt[s], Bwq[typ * 2 + half]], [Bps[pair + half]])
            rd = [Bps[pair], Bps[pair + 1]]
            if typ == 2:
                P.copy("act", vst[s][:, blk * 1024:(blk + 1) * 1024], bank2(pair), rd, [Bvst[s]])
            else:
                w = typ
                src = bank2(pair).rearrange("p (h a i) -> p h a i", h=16, a=2)
                A4 = ropeA[w].rearrange("p (h a i) -> p h a i", h=16, a=2)
                B4 = ropeB[w].rearrange("p (h a i) -> p h a i", h=16, a=2)
                P.tt("dve", A4, src, cos3[:, tb, :].unsqueeze(1).unsqueeze(1).broadcast_to([128, 16, 2, 32]), MUL, rd + [Btab], [BropeA[w]])
                P.tt("dve", B4[:, :, 0, :], src[:, :, 1, :], nsin3[:, tb, :].unsqueeze(1).broadcast_to([128, 16, 32]), MUL, rd + [Btab], [BropeB[w]])
                P.tt("dve", B4[:, :, 1, :], src[:, :, 0, :], sin3[:, tb, :].unsqueeze(1).broadcast_to([128, 16, 32]), MUL, rd + [Btab, BropeB[w]], [BropeB[w]])
                P.tt("pool", rot[w][blk % 2], ropeA[w], ropeB[w], ADD, [BropeA[w], BropeB[w]], [Brot[w][blk % 2]])
            while pendB and pendB[0][0] + 2 <= gcount[0] - 1:
                p1_B(*pendB.pop(0)[1])
            if typ != 2:
                pendB.append((gcount[0] - 1, (t, blk, typ)))

        for t in range(16):
            for blk in range(4):
                for typ in ((0, 1, 2) if t < 8 else (1, 2)):
                    p1_group(t, blk, typ)
        while pendB:
            p1_B(*pendB.pop(0)[1])
        while wconv_left:
            wconv_one()

        P.barrier()
        off[0] = base

        BATs = [Buf("ATs%d" % j) for j in range(8)]
        m2 = off[0]
        ao = [alloc(TS, BF16) for _ in range(2)]; Bao = [Buf("ao0"), Buf("ao1")]
        KT = [alloc(SEQ, BF16) for _ in range(2)]; BKTs = [Buf("KTs0"), Buf("KTs1")]
        Vh = [alloc(64 * 128, BF16) for _ in range(2)]; BVh = [Buf("Vh0"), Buf("Vh1")]
        QT = [alloc(NT, BF16) for _ in range(2)]; BQTs = [Buf("QTs0"), Buf("QTs1")]
        NPT = 4
        PT = [alloc(1024, BF16) for _ in range(NPT)]; BPT = [Buf("PT%d" % i) for i in range(NPT)]
        O1sb = alloc(512); O2sb = alloc(512); Rhi = alloc(512, BF16); Rlo = alloc(512, BF16); qhi = alloc(512, BF16); qlo = alloc(512, BF16)
        BO1sb, BO2sb, BRsb = Buf("O1sb"), Buf("O2sb"), Buf("Rsb")
        sel1b = alloc(128, BF16); sel2b = alloc(128, BF16)
        P.copy("dve", sel1b, sel1, [Bcp], [Bcp])
        P.copy("dve", sel2b, sel2, [Bcp], [Bcp])
        rl1 = alloc(512); rl2 = alloc(512); osb = alloc(512); tsb = alloc(512); asb = alloc(512); ysb = alloc(512)
        Bpp = Buf("pp")

        def load_head(h):
            s = h % 2
            P.dma("sp", QT[s], QTs[h], BQT, [BQTs[s]], "ldq%d" % s)
            P.dma("sp", KT[s], KTs[h], BKT, [BKTs[s]], "ldk%d" % s)
            P.dma("sp", Vh[s].rearrange("p (b d) -> p b d", b=64), Vs[:, :, h, :], BV, [BVh[s]], "ldv%d" % s)

        def postproc(h, j):
            B = [Bpp]
            par = (h * 8 + j) % 2

            def s0():
                ce = "act" if j <= 1 else "dve"
                P.copy(ce, O1sb, bank(4), [Bps[4]], [BO1sb])
                P.copy(ce, O2sb, bank(5), [Bps[5]], [BO2sb])
                P.copy("dve", Rhi, bank(6), [Bps[6]], [BRsb])
                P.tt("dve", Rlo, bank(6), Rhi, SUB, [Bps[6], BRsb], [BRsb])

            def s1():
                P.mm(bank(7), sel1b, Rhi, True, False, [BRsb, Bcp], [Bps[7]])
                P.mm(bank(7), sel1b, Rlo, False, True, [BRsb, Bcp], [Bps[7]])
                P.copy("dve", rl1, bank(7), [Bps[7]], B)

            def s2():
                P.mm(bank(7), sel2b, Rhi, True, False, [BRsb, Bcp], [Bps[7]])
                P.mm(bank(7), sel2b, Rlo, False, True, [BRsb, Bcp], [Bps[7]])
                P.tt("dve", osb, O1sb, bank(7), MUL, [BO1sb, Bps[7]] + B, B)
                P.tt("dve", rl2, rl1, bank(7), MUL, [Bps[7]] + B, B)
                P.tt("dve", tsb, O2sb, rl1, MUL, [BO2sb] + B, B)
                P.stt("dve", osb, tsb, neglam, osb, MUL, ADD, B + [Bc], B)
                P.tt("dve", tsb, osb, osb, MUL, B, B)
                P.copy("dve", qhi, tsb, B, B)
                P.tt("dve", qlo, tsb, qhi, SUB, B, B)
                P.stt("dve", rl2, rl2, RMS_EPS, rl2, MUL, MUL, B, B)

            def s3():
                P.mm(bank(7), ones_bf, qhi, True, False, B + [Bcp], [Bps[7]])
                P.mm(bank(7), ones_bf, qlo, False, True, B + [Bcp], [Bps[7]])
                P.stt("dve", asb, bank(7), 1.0 / 128, rl2, MUL, ADD, [Bps[7]] + B, B)
                rsqrt("dve", ysb, asb, tsb, B, 2)
                P.stt("dve", ao[par], osb, gsub, ysb, MUL, MUL, B + [Bc], [Bao[par]])
                P.dma("sp", ATs[:, h, j * TS:(j + 1) * TS], ao[par], [Bao[par]], [BATs[j]], "sta%d" % par)

            return s0, [(4, s1), (6, s2), (12, s3)]

        steps_all = []
        for h in range(8):
            for j in range(8):
                steps = []
                for kt in range(j):
                    steps += [(kt * 4 + i, 0, None, False) for i in range(4)]
                for kt in range(j):
                    steps += [(32 + kt * 4 + i, 0, None, False) for i in range(4)]
                steps += [(32 + j * 4 + i, 0, j, False) for i in range(4)]
                steps += [(j * 4 + i, 128 * i, None, True) for i in range(4)]
                for si, (kb, q0, vj, diag) in enumerate(steps):
                    steps_all.append((h, j, si, len(steps), kb, q0, vj, diag))

        def emit_qk(idx):
            h, j, si, nst, kb, q0, vj, diag = steps_all[idx]
            hs = h % 2
            sbase = (idx % 2) * 1024
            Bs = [Bps[2 * (idx % 2)], Bps[2 * (idx % 2) + 1]]
            qsl = slice(j * TS + q0, (j + 1) * TS)
            ksl = slice(kb * 128, (kb + 1) * 128)
            P.mm(psum[:, sbase + q0:sbase + 512], KT[hs][0:64, ksl], QT[hs][0:64, qsl], True, True, [BKTs[hs], BQTs[hs]], Bs)
            P.mm(psum[:, sbase + 512 + q0:sbase + 1024], KT[hs][64:128, ksl], QT[hs][64:128, qsl], True, True, [BKTs[hs], BQTs[hs]], Bs, tp=(64, 0))
            if diag:
                for m in range(2):
                    P.op("pe", lambda e, o_=psum[:, sbase + 512 * m + q0:sbase + 512 * m + q0 + 128]: e.matmul(o_, lhsT=ident, rhs=negm, start=False, stop=True, skip_group_check=True), [Bc, Bcp], Bs)

        pending = []

        def emit_rest(idx):
            h, j, si, nst, kb, q0, vj, diag = steps_all[idx]
            hs = h % 2
            pb = idx % NPT
            sbase = (idx % 2) * 1024
            Bs = [Bps[2 * (idx % 2)], Bps[2 * (idx % 2) + 1]]
            S3 = psum[:, sbase:sbase + 1024].rearrange("p (m n) -> p m n", m=2)
            PT3 = PT[pb].rearrange("p (m n) -> p m n", m=2)
            bias = 0.0 if vj is None else vis[:, vj:vj + 1]
            P.act(PT3[:, :, q0:512], S3[:, :, q0:512], AF.Exp, Bs + [Bc], [BPT[pb]], bias=bias, scale=0.125)
            first, last = si == 0, si == nst - 1
            vsl = slice(kb * 128, (kb + 1) * 128)
            if idx + 2 < NS:
                emit_qk(idx + 2)
            for m in range(2):
                P.mm(bank(4 + m)[:, q0:512], Vh[hs][:, vsl], PT3[:, m, q0:512], first, last, [BVh[hs], BPT[pb]], [Bps[4 + m]])
            if si % 2 == 1:
                for wh, sidx in ((0, idx - 1), (1, idx)):
                    _, _, si_, _, _, q0_, _, _ = steps_all[sidx]
                    PTx = PT[sidx % NPT].rearrange("p (m n) -> p m n", m=2)
                    for m in range(2):
                        ro = 64 * wh + 32 * m
                        P.mm(bank(6)[ro:ro + 32, q0_:512], ones_bf[:, 0:32], PTx[:, m, q0_:512], si_ < 2, si_ >= nst - 2, [Bcp, BPT[sidx % NPT]], [Bps[6]], tp=(0, ro))
            if last:
                while pending:
                    pending.pop(0)[1]()
                s0, rest = postproc(h, j)
                s0()
                pending.extend(rest)
                if j == 7 and h + 2 < 8:
                    load_head(h + 2)
            elif pending and si >= pending[0][0]:
                pending.pop(0)[1]()

        load_head(0)
        load_head(1)
        NS = len(steps_all)
        emit_qk(0)
        emit_qk(1)
        for i in range(NS):
            emit_rest(i)
        while pending:
            pending.pop(0)[1]()

        P.barrier()
        off[0] = m2

        NSLOT = 5
        wr = [alloc(8 * 1024, BF16) for _ in range(NSLOT)]; Bwr = [Buf("wr%d" % i) for i in range(NSLOT)]
        xt3s = [alloc(8 * TS, BF16) for _ in range(2)]; Bxt3s = [Buf("xt3_0"), Buf("xt3_1")]
        at3s = [alloc(8 * TS, BF16) for _ in range(2)]; Bat3s = [Buf("at3_0"), Buf("at3_1")]
        pTs = [alloc(2 * TS, BF16) for _ in range(2)]; BpTs = [Buf("pT0"), Buf("pT1")]
        xtok = [alloc(1024) for _ in range(2)]; Bxtok = [Buf("xtok0"), Buf("xtok1")]
        vg2 = [alloc(1024) for _ in range(2)]; Bvg2 = [Buf("vg0"), Buf("vg1")]; vg = vg2[0]; Bvg = Bvg2[0]
        vhat = alloc(4 * 1024, BF16); Bvhat = [Buf("vhat%d" % i) for i in range(4)]
        sa = alloc(8 * TS, BF16); Bsa = [Buf("sa%d" % g) for g in range(8)]
        U = alloc(8 * TS, BF16); BU = [Buf("U%d" % g) for g in range(8)]
        Zt = [alloc(TS, BF16) for _ in range(2)]; BZt = [Buf("Z0"), Buf("Z1")]
        G = [alloc(TS) for _ in range(2)]; BG = [Buf("G0"), Buf("G1")]
        mrg = alloc(8 * TS, BF16); Bmrg = [Buf("mrg%d" % d) for d in range(8)]
        sig = vg; Bsig = Bvg
        tt2 = [alloc(1024) for _ in range(2)]; Btt2 = [Buf("tt0"), Buf("tt1")]; tt_ = tt2[0]; Btt = Btt2[0]
        yy = [alloc(1024) for _ in range(2)]; Byy = [Buf("yy%d" % i) for i in range(2)]
        lngb = alloc(1024); lnbb = alloc(1024)
        Cg = alloc(8 * 128); wsTf = tt_; wsTb = alloc(8 * 128, BF16); bsb = yy[0]
        NST = 2
        stt_ = [dict(st6=alloc(12), mv=alloc(2), va=alloc(1), rs=alloc(1), tm=alloc(1), nmr=alloc(1), B=Buf("stats%d" % i)) for i in range(NST)]
        stn = [0]
        Bc3 = Buf("c3")

        P.dma("sp", lngb, lng_d.partition_broadcast(128), [], [Bc3], "c3")
        P.dma("sp", lnbb, lnb_d.partition_broadcast(128), [], [Bc3], "c3")
        P.dma("sp", wsTf.rearrange("p (g t) -> p g t", g=8), wsT_d.rearrange("g s t -> s g t"), [], [Bc3, Btt], "c3")
        P.dma("sp", bsb, bs_d.rearrange("g t -> (g t)").partition_broadcast(128), [], [Bc3, Byy[0]], "c3")
        P.tt("dve", wsTb.rearrange("p (g t) -> p g t", g=8), wsTf.rearrange("p (g t) -> p g t", g=8), trif.unsqueeze(1).broadcast_to([128, 8, 128]), MUL, [Bc3, Bc, Btt], [Bc3])
        for half in range(2):
            P.mm(bank(half), ones_bf, wsTb[:, half * 512:(half + 1) * 512], True, True, [Bc3, Bcp], [Bps[half]])
        for g in range(8):
            P.stt("dve", Cg[:, g * 128:(g + 1) * 128], psum[:, g * 128:(g + 1) * 128], alnb[:, g:g + 1], bsb[:, g * 128:(g + 1) * 128], MUL, ADD, [Bps[g // 4], Bc3, Bc, Byy[0]], [Bc3])

        nchunks = NTILE * len(CH)
        loaded = [0]

        def wload():
            n = loaded[0]
            if n >= nchunks:
                return
            loaded[0] += 1
            nm = CH[n % len(CH)]
            s = n % NSLOT
            if nm == "wp":
                P.dma("sp", wr[s][:, 0:2048].rearrange("p (k c) -> p k c", k=2), Wsc[nm].rearrange("(k p) c -> p k c", p=128), [BW[nm]], [Bwr[s]], "wr%d" % s)
            else:
                P.dma("sp", wr[s].rearrange("p (k c) -> p k c", k=8), Wsc[nm].rearrange("(k p) c -> p k c", p=128), [BW[nm]], [Bwr[s]], "wr%d" % s)

        used = [0]

        def wcur(expect, ahead=0):
            n = used[0] + ahead
            assert CH[n % len(CH)] == expect
            s = n % NSLOT
            return wr[s].rearrange("p (k c) -> p k c", k=8), Bwr[s]

        def wdone():
            used[0] += 1
            wload()

        rr1 = [0]
        rr2 = [0]

        def nb1():
            b = (4 + rr1[0]) % 8
            rr1[0] += 1
            return b

        def nb2(nset=2):
            b = 2 * (rr2[0] % nset)
            rr2[0] += 1
            return b

        def load_tile_inputs(j):
            s_ = j % 2
            tsl_ = slice(j * TS, (j + 1) * TS)
            P.dma("sp", xt3s[s_].rearrange("p (k n) -> p k n", k=8), xTb[:, :, tsl_], [BxTb[j]], [Bxt3s[s_]], "xt3_%d" % s_)
            P.dma("pool", pTs[s_].rearrange("p (k n) -> p k n", k=2), pT_d.rearrange("(k p) n -> p k n", p=128)[:, :, tsl_], [], [BpTs[s_]], "pT%d" % s_)
            P.dma("sp", at3s[s_].rearrange("p (h n) -> p h n", h=8), ATs[:, :, tsl_], [BATs[j]], [Bat3s[s_]], "at3_%d" % s_)

        def load_xtok(gb):
            xs_ = gb % 2
            P.dma("sp", xtok[xs_], x_own[gb * 128:(gb + 1) * 128, :], [], [Bxtok[xs_]], "xtok%d" % xs_)

        load_tile_inputs(0)
        for _ in range(NSLOT):
            wload()
        load_xtok(0)

        U3 = U.rearrange("p (g n) -> p g n", g=8)
        sa3 = sa.rearrange("p (g n) -> p g n", g=8)
        mrg3 = mrg.rearrange("p (g n) -> p g n", g=8)
        vhat3 = vhat.rearrange("p (b c) -> p b c", b=4)
        out_ops = []

        def layernorm(src, Bsrc, dst, Bdst):
            S_ = stt_[stn[0] % NST]
            stn[0] += 1
            B_ = [S_["B"]]
            st6, mv, va_, rs_, tm_, nmr = S_["st6"], S_["mv"], S_["va"], S_["rs"], S_["tm"], S_["nmr"]
            P.op("dve", lambda e: e.bn_stats(out=st6[:, 0:6], in_=src[:, 0:512]), [Bsrc], B_)
            P.op("dve", lambda e: e.bn_stats(out=st6[:, 6:12], in_=src[:, 512:1024]), [Bsrc] + B_, B_)
            P.op("dve", lambda e: e.bn_aggr(out=mv, in_=st6), B_, B_)
            P.ts("dve", va_, mv[:, 1:2], LN_EPS, None, ADD, None, B_, B_)
            rsqrt("dve", rs_, va_, tm_, B_, 2)
            P.stt("dve", nmr, mv[:, 0:1], -1.0, rs_, MUL, MUL, B_, B_)
            P.act(dst, src, AF.Identity, [Bsrc] + B_, [Bdst], bias=nmr, scale=rs_)

        def fm_proj(wname, dst_fn):
            w3, Bw = wcur(wname)
            for dc in range(8):
                bk = nb1()
                for kc in range(8):
                    P.mm(bank(bk), w3[:, kc, dc * 128:(dc + 1) * 128], x3[:, kc, :], kc == 0, kc == 7, [Bw, Bxt3], [Bps[bk]])
                dst_fn(dc, bk)
            wdone()

        def gated_proj(wname, gname, rhs_fn, Brhs_fn, combine):
            w3a, Bwa = wcur(wname)
            w3g, Bwg = wcur(gname, 1)
            for dc in range(8):
                bka = nb1()
                for kc in range(8):
                    P.mm(bank(bka), w3a[:, kc, dc * 128:(dc + 1) * 128], rhs_fn(kc), kc == 0, kc == 7, [Bwa, Brhs_fn(kc)], [Bps[bka]])
                bkg = nb1()
                for kc in range(8):
                    P.mm(bank(bkg), w3g[:, kc, dc * 128:(dc + 1) * 128], x3[:, kc, :], kc == 0, kc == 7, [Bwg, Bxt3], [Bps[bkg]])
                z = dc % 2
                P.act(G[z], bank(bkg), AF.Sigmoid, [Bps[bkg]], [BG[z]])
                combine(dc, bka, z)
            wdone()
            wdone()

        for j in range(NTILE):
            ts_ = j % 2
            x3 = xt3s[ts_].rearrange("p (k n) -> p k n", k=8); Bxt3 = Bxt3s[ts_]
            p3 = pTs[ts_].rearrange("p (k n) -> p k n", k=2); BpT = BpTs[ts_]
            attn3 = at3s[ts_].rearrange("p (h n) -> p h n", h=8); Bat3 = Bat3s[ts_]
            if j + 1 < NTILE:
                load_tile_inputs(j + 1)
            w3, Bw = wcur("va")
            for blk in range(4):
                b0 = nb2()
                for half in range(2):
                    for kc in range(8):
                        P.mm(bank(b0 + half), x3[:, kc, blk * 128:(blk + 1) * 128], w3[:, kc, half * 512:(half + 1) * 512], kc == 0, kc == 7, [Bw, Bxt3], [Bps[b0 + half]])
                P.act(vg2[blk % 2], bank2(b0), AF.Gelu, [Bps[b0], Bps[b0 + 1]], [Bvg2[blk % 2]])
                layernorm(vg2[blk % 2], Bvg2[blk % 2], vhat3[:, blk, :], Bvhat[blk])
            wdone()

            def ua_dst(dc, bk):
                P.act(U3[:, dc, :], bank(bk), AF.Gelu, [Bps[bk]], [BU[dc]])
            fm_proj("ua", ua_dst)
            for g in range(8):
                bk = nb1()
                for blk in range(4):
                    P.mm(bank(bk)[:, blk * 128:(blk + 1) * 128], vhat3[:, blk, g * 128:(g + 1) * 128], wsTb[:, g * 128:(g + 1) * 128], True, True, [Bvhat[blk], Bc3], [Bps[bk]])
                P.stt("dve", sa3[:, g, :].rearrange("p (b t) -> p b t", b=4), bank(bk).rearrange("p (b t) -> p b t", b=4), alng[:, g:g + 1],
                      Cg[:, g * 128:(g + 1) * 128].unsqueeze(1).broadcast_to([128, 4, 128]), MUL, ADD, [Bps[bk], Bc3, Bc], [Bsa[g]])
            for dc in range(8):
                P.tt("pool", U3[:, dc, :], U3[:, dc, :], sa3[:, dc, :], MUL, [BU[dc], Bsa[dc]], [BU[dc]])

            def za_dst(dc, bk):
                z = dc % 2
                P.act(Zt[z], bank(bk), AF.Silu, [Bps[bk]], [BZt[z]])
                P.tt("pool", U3[:, dc, :], U3[:, dc, :], Zt[z], MUL, [BU[dc], BZt[z]], [BU[dc]])
            fm_proj("za", za_dst)

            def comb_a(dc, bka, z):
                P.tt("dve", mrg3[:, dc, :], bank(bka), G[z], MUL, [Bps[bka], BG[z]], [Bmrg[dc]])
            gated_proj("wa", "ga", lambda kc: U3[:, kc, :], lambda kc: BU[kc], comb_a)

            def zb_dst(dc, bk):
                z = dc % 2
                P.act(Zt[z], bank(bk), AF.Silu, [Bps[bk]], [BZt[z]])
                P.tt("pool", attn3[:, dc, :], attn3[:, dc, :], Zt[z], MUL, [Bat3, BZt[z]], [Bat3])
            fm_proj("zb", zb_dst)

            def comb_b(dc, bkb, z):
                P.tt("dve", G[z], bank(bkb), G[z], MUL, [Bps[bkb], BG[z]], [BG[z]])
                P.tt("pool", mrg3[:, dc, :], mrg3[:, dc, :], G[z], ADD, [Bmrg[dc], BG[z]], [Bmrg[dc]])
            gated_proj("wb", "gb", lambda kc: attn3[:, kc, :], lambda kc: Bat3, comb_b)

            w3o, Bwo = wcur("wout")
            w3pg, Bwpg = wcur("wpg", 1)
            w3p, Bwp = wcur("wp", 2)
            for blk in range(4):
                gb = j * 4 + blk
                r0 = gb * 128
                bsl = slice(blk * 128, (blk + 1) * 128)
                xs = gb % 2
                ys = gb % 2
                tt_ = tt2[gb % 2]; Btt = Btt2[gb % 2]
                sig = vg2[gb % 2]; Bsig = Bvg2[gb % 2]
                if gb + 1 < 32:
                    load_xtok(gb + 1)
                bpg = nb2(4)
                for half in range(2):
                    for kc in range(8):
                        P.mm(bank(bpg + half), x3[:, kc, bsl], w3pg[:, kc, half * 512:(half + 1) * 512], kc == 0, kc == 7, [Bwpg, Bxt3], [Bps[bpg + half]])
                P.act(sig, bank2(bpg), AF.Sigmoid, [Bps[bpg], Bps[bpg + 1]], [Bsig])
                bpw = nb2(4)
                for half in range(2):
                    for kc in range(2):
                        P.mm(bank(bpw + half), p3[:, kc, bsl], w3p[:, kc, half * 512:(half + 1) * 512], kc == 0, kc == 1, [Bwp, BpT], [Bps[bpw + half]])
                P.tt("dve", tt_, bank2(bpw), sig, MUL, [Bps[bpw], Bps[bpw + 1], Bsig], [Btt])
                P.stt("dve", tt_, xtok[xs], ALPHA, tt_, MUL, ADD, [Bxtok[xs], Btt], [Btt])
                bmx = nb2(4)
                for half in range(2):
                    for kc in range(8):
                        P.mm(bank(bmx + half), mrg3[:, kc, bsl], w3o[:, kc, half * 512:(half + 1) * 512], kc == 0, kc == 7, [Bwo, Bmrg[kc]], [Bps[bmx + half]])
                P.tt("dve", tt_, bank2(bmx), tt_, ADD, [Bps[bmx], Bps[bmx + 1], Btt], [Btt])
                layernorm(tt_, Btt, yy[ys], Byy[ys])
                P.tt("pool", yy[ys], yy[ys], lngb, MUL, [Byy[ys], Bc3], [Byy[ys]])
                P.tt("pool", yy[ys], yy[ys], lnbb, ADD, [Byy[ys], Bc3], [Byy[ys]])
                out_ops.append(P.dma("sp", out_d[r0:r0 + 128, :], yy[ys], [Byy[ys]], [], "ost%d" % ys))
            wdone()
            wdone()
            wdone()

        P.emit(nc, final_waits=out_ops)
    return nc


_NC_CACHE = {}


def _core_layout(core):
    b, r = core // 2, core % 2
    own = [2 * j + ((j & 1) ^ r) for j in range(8)]
    oth = [2 * j + 1 - ((j & 1) ^ r) for j in range(8)]
    tok_own = np.concatenate([np.arange(g * TS, (g + 1) * TS) for g in own])
    tok_oth = np.concatenate([np.arange(g * TS, (g + 1) * TS) for g in oth])
    vis = np.array([0.0 if ((j & 1) ^ r) == 1 else NEG for j in range(8)], dtype=np.float32)
    return b, tok_own, tok_oth, vis


def kernel(x, p, positions, w_in, a_ln_g, a_ln_b, a_w_s, a_b_s, b_lam_q1, b_lam_k1, b_lam_q2, b_lam_k2,
           b_subln_g, w_branch_a, w_branch_b, w_out, w_ple, w_ple_gate, ln_g, ln_b):
    f = lambda a: np.ascontiguousarray(np.asarray(a))
    x = f(x); p = f(p); positions = f(positions)
    if "nc" not in _NC_CACHE:
        _NC_CACHE["nc"] = build_program()
    nc = _NC_CACHE["nc"]
    tri = np.triu(np.ones((128, 128), dtype=np.float32))
    invf = (np.float32(10000.0) ** (-(np.arange(0, 64, 2, dtype=np.float32)) / np.float32(64))).astype(np.float32)
    shared = {
        "w_in": f(w_in[0]), "w_a": f(w_branch_a[0]), "w_b": f(w_branch_b[0]), "w_out": f(w_out[0]),
        "w_pg": f(w_ple_gate[0]), "w_p": f(w_ple[0]),
        "wsT": f(np.transpose(np.asarray(a_w_s[0]), (0, 2, 1))),
        "tri": tri, "bs": f(a_b_s[0]),
        "alng": f(np.asarray(a_ln_g[0]).reshape(8, 128).T), "alnb": f(np.asarray(a_ln_b[0]).reshape(8, 128).T),
        "subg": f(np.asarray(b_subln_g[0]).reshape(128, 1)),
        "lamv": f(np.stack([np.asarray(b_lam_q1[0]), np.asarray(b_lam_k1[0]), np.asarray(b_lam_q2[0]), np.asarray(b_lam_k2[0])])),
        "lng": f(ln_g[0]), "lnb": f(ln_b[0]),
        "invf": f(np.broadcast_to(invf[None, :], (128, 32))),
    }
    in_maps = []
    layouts = []
    for c in range(NCORES):
        b, tok_own, tok_oth, vis = _core_layout(c)
        tok_all = np.concatenate([tok_own, tok_oth])
        m = dict(shared)
        m["xT_all"] = f(x[b][tok_all].T)
        m["x_own"] = f(x[b][tok_own])
        m["pT"] = f(p[0, b][tok_own].T)
        m["pos"] = f(positions[b][tok_all].reshape(64, 128).T.astype(np.int32))
        m["vis"] = f(np.broadcast_to(vis[None, :], (128, 8)))
        in_maps.append(m)
        layouts.append((b, tok_own))
    res = run_bass_kernel_spmd(nc, in_maps, core_ids=list(range(NCORES)))
    out = np.empty((4, SEQ, D), dtype=np.float32)
    for c in range(NCORES):
        b, tok_own = layouts[c]
        out[b, tok_own] = res.results[c]["out"]
    return out
```

```python
import bisect
import contextlib
import math

import numpy as np
import concourse.bass as bass
import concourse.mybir as mybir
from concourse.bass_utils import run_bass_kernel_spmd

F32 = mybir.dt.float32
BF16 = mybir.dt.bfloat16
I32 = mybir.dt.int32
AF = mybir.ActivationFunctionType
ALU = mybir.AluOpType

NCORES = 8
D = 1024
SEQ = 8192
NT = 4096
TS = 512
NTILE = NT // TS
LN_EPS = 1e-5
RMS_EPS = 1e-5
ALPHA = 2.0 ** 0.25
LAM_INIT = 0.8 - 0.6 * math.exp(0.0)
NEG = -30000.0


class Buf:
    __slots__ = ("name", "w", "r", "rd")

    def __init__(self, name):
        self.name = name
        self.w = None
        self.r = {}
        self.rd = []


class Op:
    __slots__ = ("eng", "fn", "seq", "deps", "inc", "key", "cnt")


class Prog:
    ENG = ("pe", "act", "dve", "pool", "sp")

    def __init__(self):
        self.ops = []
        self.by_eng = {e: [] for e in self.ENG}
        self.key_seqs = {}
        self.key_last = {}
        self.pending = {e: [] for e in self.ENG}

    def op(self, eng, fn, reads=(), writes=(), key=None):
        o = Op()
        o.eng, o.fn, o.seq, o.inc, o.key, o.cnt = eng, fn, len(self.ops), False, key, 0
        deps = []
        for b in reads:
            if b.w is not None:
                deps.append(b.w)
        for b in writes:
            if b.w is not None:
                deps.append(b.w)
            deps.extend(b.r.values())
            deps.extend(b.rd)
        if self.pending[eng]:
            deps.extend(self.pending[eng])
            self.pending[eng] = []
        for b in reads:
            if key is None:
                b.r[eng] = o
            else:
                b.rd.append(o)
        for b in writes:
            b.w = o
            b.r = {}
            b.rd = []
        o.deps = [d for d in dict.fromkeys(deps)
                  if d is not o and not (d.eng == "pe" and eng == "pe" and d.key is None and key is None)]
        for d in o.deps:
            d.inc = True
        self.ops.append(o)
        self.by_eng[eng].append(o)
        if key is not None:
            self.key_seqs.setdefault(key, []).append(o.seq)
            self.key_last[key] = o
        return o

    def barrier(self):
        last = []
        for e in self.ENG:
            for o in reversed(self.by_eng[e]):
                if o.key is None:
                    last.append(o)
                    break
        last.extend(self.key_last.values())
        for o in last:
            o.inc = True
        for e in self.ENG:
            self.pending[e] = list(last)


    def mm(self, out, lhsT, rhs, start, stop, reads, writes, tp=None):
        kw = {} if tp is None else {"tile_position": tp}
        return self.op("pe", lambda e: e.matmul(out, lhsT=lhsT, rhs=rhs, start=start, stop=stop, **kw), reads, writes)

    def act(self, out, in_, func, reads, writes, bias=None, scale=None):
        kw = {}
        if bias is not None:
            kw["bias"] = bias
        if scale is not None:
            kw["scale"] = scale
        return self.op("act", lambda e: e.activation(out=out, in_=in_, func=func, **kw), reads, writes)

    def copy(self, eng, out, in_, reads, writes):
        if eng == "act":
            fn = lambda e: e.copy(out=out, in_=in_)
        else:
            fn = lambda e: e.tensor_copy(out=out, in_=in_)
        return self.op(eng, fn, reads, writes)

    def tt(self, eng, out, in0, in1, op, reads, writes):
        return self.op(eng, lambda e: e.tensor_tensor(out=out, in0=in0, in1=in1, op=op), reads, writes)

    def ts(self, eng, out, in0, s1, s2, op0, op1, reads, writes):
        if op1 is None:
            fn = lambda e: e.tensor_scalar(out=out, in0=in0, scalar1=s1, scalar2=None, op0=op0)
        else:
            fn = lambda e: e.tensor_scalar(out=out, in0=in0, scalar1=s1, scalar2=s2, op0=op0, op1=op1)
        return self.op(eng, fn, reads, writes)

    def stt(self, eng, out, in0, scalar, in1, op0, op1, reads, writes):
        return self.op(eng, lambda e: e.scalar_tensor_tensor(out=out, in0=in0, scalar=scalar, in1=in1, op0=op0, op1=op1), reads, writes)

    def dma(self, q, out, in_, reads, writes, key):
        return self.op(q, lambda e: e.dma_start(out=out, in_=in_), reads, writes, key=key)

    def memset(self, eng, out, val, reads, writes):
        return self.op(eng, lambda e: e.memset(out, val), reads, writes)

    def emit(self, nc, final_waits=()):
        keys = list(self.key_seqs)
        with contextlib.ExitStack() as st:
            esem = {e: st.enter_context(nc.semaphore("s_" + e)) for e in self.ENG}
            ksem = {k: st.enter_context(nc.semaphore("k_%s" % (k,))) for k in keys}
            for e in self.ENG:
                c = 0
                for o in self.by_eng[e]:
                    if o.key is None and o.inc:
                        c += 1
                        o.cnt = c
            block = st.enter_context(nc.Block())
            engobj = {"pe": "tensor", "act": "scalar", "dve": "vector", "pool": "gpsimd", "sp": "sync"}

            def run(e, eng):
                waited = {}

                def need(d, seq):
                    if d.key is None:
                        return ("e", d.eng), esem[d.eng], d.cnt
                    return ("k", d.key), ksem[d.key], 16 * bisect.bisect_left(self.key_seqs[d.key], seq)

                def dowait(k, s, v):
                    if waited.get(k, 0) < v:
                        waited[k] = v
                        eng.wait_ge(s, v)

                for o in self.by_eng[e]:
                    for d in o.deps:
                        dowait(*need(d, o.seq))
                    ins = o.fn(eng)
                    if o.key is not None:
                        ins.then_inc(ksem[o.key], 16)
                    elif o.inc:
                        ins.then_inc(esem[e], 1)
                if e == "sp":
                    for d in final_waits:
                        dowait(*need(d, len(self.ops)))

            for e in self.ENG:
                getattr(block, engobj[e])(lambda eng, e=e: run(e, eng))


IN_OFF = {"ua": 0, "va": 1024, "za": 2048, "q": 3072, "k": 4096, "v": 5120, "zb": 6144, "ga": 7168, "gb": 8192}
CH = ["va", "ua", "za", "zb", "wa", "ga", "wb", "gb", "wout", "wpg", "wp"]


def build_program():
    nc = bass.Bass("TRN2", target_bir_lowering=False)
    dram_in = lambda n, s, d=F32: nc.dram_tensor(n, list(s), d, kind="ExternalInput").ap()
    xT_all = dram_in("xT_all", [D, SEQ])
    x_own = dram_in("x_own", [NT, D])
    pT_d = dram_in("pT", [256, NT])
    pos_d = dram_in("pos", [128, 64], I32)
    w_in = dram_in("w_in", [D, 9216])
    w_a = dram_in("w_a", [D, D])
    w_b = dram_in("w_b", [D, D])
    w_out = dram_in("w_out", [D, D])
    w_pg = dram_in("w_pg", [D, D])
    w_p = dram_in("w_p", [256, D])
    wsT_d = dram_in("wsT", [8, 128, 128])
    tri_d = dram_in("tri", [128, 128])
    bs_d = dram_in("bs", [8, 128])
    alng_d = dram_in("alng", [128, 8])
    alnb_d = dram_in("alnb", [128, 8])
    subg_d = dram_in("subg", [128, 1])
    lamv_d = dram_in("lamv", [4, 64])
    lng_d = dram_in("lng", [D])
    lnb_d = dram_in("lnb", [D])
    vis_d = dram_in("vis", [128, 8])
    invf_d = dram_in("invf", [128, 32])
    out_d = nc.dram_tensor("out", [NT, D], F32, kind="ExternalOutput").ap()

    scr = lambda n, s: nc.dram_tensor(n, list(s), BF16, kind="Internal").ap()
    KTs = scr("KTs", [8, 128, SEQ])
    Vs = scr("Vs", [128, 64, 8, 128])
    QTs = scr("QTs", [8, 128, NT])
    xTb = scr("xTb", [128, 8, NT])
    ATs = scr("ATs", [128, 8, NT])
    Wsc = {n: scr("W_" + n, [256 if n == "wp" else D, D]) for n in CH}
    wsrc = {"wa": w_a, "wb": w_b, "wout": w_out, "wpg": w_pg, "wp": w_p}
    for n in ("va", "ua", "za", "ga", "zb", "gb"):
        wsrc[n] = w_in[:, IN_OFF[n]:IN_OFF[n] + 1024]

    P = Prog()
    MUL, ADD, SUB = ALU.mult, ALU.add, ALU.subtract
    with contextlib.ExitStack() as st:
        ARENA_COLS = 53000
        arena = st.enter_context(nc.sbuf_tensor("arena", [128, ARENA_COLS], F32))
        abf = arena.bitcast(BF16)
        ai32 = arena.bitcast(I32)
        psum = st.enter_context(nc.psum_tensor("psum", [128, 4096], F32))
        off = [0]

        def alloc(cols, dt=F32):
            nb = cols * (2 if dt == BF16 else 4)
            nb = (nb + 63) // 64 * 64
            o = off[0]
            off[0] += nb
            assert off[0] <= ARENA_COLS * 4, "SBUF arena overflow %d" % off[0]
            if dt == BF16:
                return abf[:, o // 2:o // 2 + cols]
            if dt == I32:
                return ai32[:, o // 4:o // 4 + cols]
            return arena[:, o // 4:o // 4 + cols]

        Bps = [Buf("ps%d" % i) for i in range(8)]
        bank = lambda i: psum[:, i * 512:(i + 1) * 512]
        bank2 = lambda i: psum[:, i * 512:(i + 2) * 512]

        def rsqrt(eng, out, a, tmp, B, iters):
            oi, ai = out.bitcast(I32), a.bitcast(I32)
            P.ts(eng, oi, ai, 1, None, ALU.arith_shift_right, None, B, B)
            P.ts(eng, oi, oi, -1, 0x5f3759df, MUL, ADD, B, B)
            for _ in range(iters):
                P.stt(eng, tmp, out, -0.5, out, MUL, MUL, B, B)
                P.tt(eng, tmp, tmp, a, MUL, B, B)
                P.stt(eng, out, tmp, 1.5, out, ADD, MUL, B, B)

        Bc = Buf("consts")
        identf = alloc(128); ident = alloc(128, BF16)
        onesf = alloc(128); ones_bf = alloc(128, BF16)
        trif = alloc(128); tri_bf = alloc(128, BF16)
        sel1 = alloc(128); sel2 = alloc(128)
        vis = alloc(8); subg = alloc(1); gsub = alloc(1); neglam = alloc(1)
        lamt = alloc(256); lamp = alloc(128); lams = alloc(2)
        invf = alloc(32)
        alng = alloc(8); alnb = alloc(8)
        posi = alloc(64, I32); posf = alloc(64)
        for o_, i_ in ((trif, tri_d), (vis, vis_d), (subg, subg_d), (invf, invf_d), (alng, alng_d), (alnb, alnb_d), (posi, pos_d),
                       (lamt, lamv_d.rearrange("a b -> (a b)").partition_broadcast(128))):
            P.dma("sp", o_, i_, [], [Bc], "c")
        P.copy("dve", tri_bf, trif, [Bc], [Bc])
        negm = alloc(128, BF16)
        P.ts("dve", negm, trif, -1.0, -NEG, ADD, MUL, [Bc], [Bc])
        P.tt("dve", lamp[:, 0:64], lamt[:, 0:64], lamt[:, 64:128], MUL, [Bc], [Bc])
        P.tt("dve", lamp[:, 64:128], lamt[:, 128:192], lamt[:, 192:256], MUL, [Bc], [Bc])
        P.op("dve", lambda e: e.reduce_sum(out=lams, in_=lamp.rearrange("p (a b) -> p a b", a=2), axis=mybir.AxisListType.X), [Bc], [Bc])
        P.act(lams, lams, AF.Exp, [Bc], [Bc])
        P.tt("dve", neglam, lams[:, 1:2], lams[:, 0:1], SUB, [Bc], [Bc])
        P.ts("dve", neglam, neglam, -LAM_INIT, None, ADD, None, [Bc], [Bc])
        P.ts("dve", gsub, subg, 1.0 - LAM_INIT, None, MUL, None, [Bc], [Bc])
        base = off[0]

        BW = {n: Buf("W_" + n) for n in CH}

        wqkv = alloc(8 * 3072, BF16); Bwq = [Buf("wqkv%d" % i) for i in range(6)]
        wq3 = wqkv.rearrange("p (k c) -> p k c", k=8)
        xt = [alloc(8 * TS, BF16) for _ in range(2)]; Bxt = [Buf("xt0"), Buf("xt1")]
        cosT = alloc(2048); sinT = alloc(2048); nsinT = alloc(2048); Btab = Buf("tab")
        ropeA = [alloc(1024) for _ in range(2)]; ropeB = [alloc(1024) for _ in range(2)]
        BropeA = [Buf("ropeA0"), Buf("ropeA1")]; BropeB = [Buf("ropeB0"), Buf("ropeB1")]
        rot = [[alloc(1024, BF16) for _ in range(2)] for _ in range(2)]; Brot = [[Buf("rot%d%d" % (w, b)) for b in range(2)] for w in range(2)]
        qst = [alloc(8 * TS, BF16) for _ in range(2)]; Bqst = [Buf("qst0"), Buf("qst1")]
        kst = [alloc(8 * TS, BF16) for _ in range(2)]; Bkst = [Buf("kst0"), Buf("kst1")]
        vst = [alloc(4 * 1024, BF16) for _ in range(2)]; Bvst = [Buf("vst0"), Buf("vst1")]
        ang = alloc(2048); kf = alloc(2048); ki = alloc(2048, I32); msk = alloc(2048)

        xT_v = xT_all.rearrange("(k p) n -> p k n", p=128)

        def load_wq(gi):
            c0 = IN_OFF["q"] + gi * 512
            P.dma("pool", wq3[:, :, gi * 512:(gi + 1) * 512], w_in[:, c0:c0 + 512].rearrange("(k p) c -> p k c", p=128), [], [Bwq[gi]], "wqkv")

        def load_xt(t):
            s = t % 2
            P.dma("pool", xt[s].rearrange("p (k n) -> p k n", k=8), xT_v[:, :, t * TS:(t + 1) * TS], [], [Bxt[s]], "xt%d" % s)

        load_wq(0)
        load_xt(0)
        load_wq(1)
        Bcp = Buf("consts_pool")
        P.memset("pool", identf, 1.0, [], [Bcp])
        P.op("pool", lambda e: e.affine_select(out=identf, in_=identf, pattern=[[-1, 128]], compare_op=ALU.is_equal, fill=0.0, base=0, channel_multiplier=1), [Bcp], [Bcp])
        P.copy("pool", ident, identf, [Bcp], [Bcp])
        P.memset("pool", onesf, 1.0, [], [Bcp])
        P.memset("pool", ones_bf, 1.0, [], [Bcp])
        P.memset("pool", sel1, 0.0, [], [Bcp])
        P.memset("pool", sel2, 0.0, [], [Bcp])
        P.memset("pool", sel1[0:1, :], 1.0, [Bcp], [Bcp])
        P.memset("pool", sel1[64:65, :], 1.0, [Bcp], [Bcp])
        P.memset("pool", sel2[32:33, :], 1.0, [Bcp], [Bcp])
        P.memset("pool", sel2[96:97, :], 1.0, [Bcp], [Bcp])
        for gi in range(2, 6):
            load_wq(gi)
        load_xt(1)
        wconv_left = list(CH)

        def wconv_one():
            if wconv_left:
                n = wconv_left.pop(0)
                P.dma("pool", Wsc[n], wsrc[n], [], [BW[n]], "wconv")

        Bt_ = [Btab, Bc]
        P.copy("dve", posf, posi, [Bc], [Bc])
        P.tt("dve", ang.rearrange("p (b i) -> p b i", b=64), posf.unsqueeze(2).broadcast_to([128, 64, 32]), invf.unsqueeze(1).broadcast_to([128, 64, 32]), MUL, [Bc], [Btab])
        C1 = 6.28125
        C2 = 2 * math.pi - 6.28125

        def wrap(t):
            P.ts("dve", msk, t, math.pi, -2 * math.pi, ALU.is_gt, MUL, Bt_, Bt_)
            P.tt("dve", t, t, msk, ADD, Bt_, Bt_)
            P.ts("dve", msk, t, -math.pi, 2 * math.pi, ALU.is_lt, MUL, Bt_, Bt_)
            P.tt("dve", t, t, msk, ADD, Bt_, Bt_)

        P.ts("dve", kf, ang, 1.0 / (2 * math.pi), None, MUL, None, Bt_, Bt_)
        P.copy("dve", ki, kf, Bt_, Bt_)
        P.copy("dve", kf, ki, Bt_, Bt_)
        P.stt("dve", ang, kf, -C1, ang, MUL, ADD, Bt_, Bt_)
        P.stt("dve", ang, kf, -C2, ang, MUL, ADD, Bt_, Bt_)
        wrap(ang)
        P.act(sinT, ang, AF.Sin, Bt_, Bt_)
        P.ts("dve", ang, ang, math.pi / 2, None, ADD, None, Bt_, Bt_)
        wrap(ang)
        P.act(cosT, ang, AF.Sin, Bt_, Bt_)
        P.ts("dve", nsinT, sinT, -1.0, None, MUL, None, Bt_, Bt_)

        BKT = [Buf("KT%d" % t) for t in range(16)]
        BV = [Buf("V%d" % t) for t in range(16)]
        BQT = [Buf("QT%d" % t) for t in range(8)]
        BxTb = [Buf("xTb%d" % t) for t in range(8)]
        cos3 = cosT.rearrange("p (b i) -> p b i", b=64)
        sin3 = sinT.rearrange("p (b i) -> p b i", b=64)
        nsin3 = nsinT.rearrange("p (b i) -> p b i", b=64)
        gcount = [0]
        pendB = []

        def p1_store(t):
            s = t % 2
            if t < 8:
                P.dma("sp", xTb[:, :, t * TS:(t + 1) * TS], xt[s].rearrange("p (k n) -> p k n", k=8), [Bxt[s]], [BxTb[t]], "stx%d" % s)
                P.dma("sp", QTs.rearrange("h p n -> p h n")[:, :, t * TS:(t + 1) * TS], qst[s].rearrange("p (h n) -> p h n", h=8), [Bqst[s]], [BQT[t]], "stq%d" % s)
            P.dma("sp", KTs.rearrange("h p n -> p h n")[:, :, t * TS:(t + 1) * TS], kst[s].rearrange("p (h n) -> p h n", h=8), [Bkst[s]], [BKT[t]], "stk%d" % s)
            P.dma("sp", Vs[:, t * 4:(t + 1) * 4, :, :].rearrange("p b h d -> p b (h d)"), vst[s].rearrange("p (b c) -> p b c", b=4), [Bvst[s]], [BV[t]], "stv%d" % s)

        def p1_B(t, blk, which):
            s = t % 2
            stage, Bstage = (qst[s], Bqst[s]) if which == 0 else (kst[s], Bkst[s])
            st3 = stage.rearrange("p (h n) -> p h n", h=8)
            for half in range(2):
                bk = 6 + half
                for hh in range(4):
                    h = half * 4 + hh
                    P.mm(bank(bk)[:, hh * 128:(hh + 1) * 128], rot[which][blk % 2][:, h * 128:(h + 1) * 128], ident, True, True, [Brot[which][blk % 2], Bcp], [Bps[bk]])
                P.copy("act", st3[:, half * 4:(half + 1) * 4, blk * 128:(blk + 1) * 128], bank(bk).rearrange("p (h n) -> p h n", h=4), [Bps[bk]], [Bstage])
            if blk == 3 and which == 1:
                p1_store(t)
                if t + 2 < 16:
                    load_xt(t + 2)
                wconv_one()

        def p1_group(t, blk, typ):
            s = t % 2
            tb = t * 4 + blk
            x3 = xt[s].rearrange("p (k n) -> p k n", k=8)
            pair = (gcount[0] % 3) * 2
            gcount[0] += 1
            for half in range(2):
                c0 = typ * 1024 + half * 512
                for kc in range(8):
                    P.mm(bank(pair + half), x3[:, kc, blk * 128:(blk + 1) * 128], wq3[:, kc, c0:c0 + 512], kc == 0, kc == 7, [Bxt[s], Bwq[typ * 2 + half]], [Bps[pair + half]])
            rd = [Bps[pair], Bps[pair + 1]]
            if typ == 2:
                P.copy("act", vst[s][:, blk * 1024:(blk + 1) * 1024], bank2(pair), rd, [Bvst[s]])
            else:
                w = typ
                src = bank2(pair).rearrange("p (h a i) -> p h a i", h=16, a=2)
                A4 = ropeA[w].rearrange("p (h a i) -> p h a i", h=16, a=2)
                B4 = ropeB[w].rearrange("p (h a i) -> p h a i", h=16, a=2)
                P.tt("dve", A4, src, cos3[:, tb, :].unsqueeze(1).unsqueeze(1).broadcast_to([128, 16, 2, 32]), MUL, rd + [Btab], [BropeA[w]])
                P.tt("dve", B4[:, :, 0, :], src[:, :, 1, :], nsin3[:, tb, :].unsqueeze(1).broadcast_to([128, 16, 32]), MUL, rd + [Btab], [BropeB[w]])
                P.tt("dve", B4[:, :, 1, :], src[:, :, 0, :], sin3[:, tb, :].unsqueeze(1).broadcast_to([128, 16, 32]), MUL, rd + [Btab, BropeB[w]], [BropeB[w]])
                P.tt("pool", rot[w][blk % 2], ropeA[w], ropeB[w], ADD, [BropeA[w], BropeB[w]], [Brot[w][blk % 2]])
            while pendB and pendB[0][0] + 2 <= gcount[0] - 1:
                p1_B(*pendB.pop(0)[1])
            if typ != 2:
                pendB.append((gcount[0] - 1, (t, blk, typ)))

        for t in range(16):
            for blk in range(4):
                for typ in ((0, 1, 2) if t < 8 else (1, 2)):
                    p1_group(t, blk, typ)
        while pendB:
            p1_B(*pendB.pop(0)[1])
        while wconv_left:
            wconv_one()

        P.barrier()
        off[0] = base

        BATs = [Buf("ATs%d" % j) for j in range(8)]
        m2 = off[0]
        ao = [alloc(TS, BF16) for _ in range(2)]; Bao = [Buf("ao0"), Buf("ao1")]
        KT = [alloc(SEQ, BF16) for _ in range(2)]; BKTs = [Buf("KTs0"), Buf("KTs1")]
        Vh = [alloc(64 * 128, BF16) for _ in range(2)]; BVh = [Buf("Vh0"), Buf("Vh1")]
        QT = [alloc(NT, BF16) for _ in range(2)]; BQTs = [Buf("QTs0"), Buf("QTs1")]
        NPT = 4
        PT = [alloc(1024, BF16) for _ in range(NPT)]; BPT = [Buf("PT%d" % i) for i in range(NPT)]
        O1sb = alloc(512); O2sb = alloc(512); Rhi = alloc(512, BF16); Rlo = alloc(512, BF16); qhi = alloc(512, BF16); qlo = alloc(512, BF16)
        BO1sb, BO2sb, BRsb = Buf("O1sb"), Buf("O2sb"), Buf("Rsb")
        sel1b = alloc(128, BF16); sel2b = alloc(128, BF16)
        P.copy("dve", sel1b, sel1, [Bcp], [Bcp])
        P.copy("dve", sel2b, sel2, [Bcp], [Bcp])
        rl1 = alloc(512); rl2 = alloc(512); osb = alloc(512); tsb = alloc(512); asb = alloc(512); ysb = alloc(512)
        Bpp = Buf("pp")

        def load_head(h):
            s = h % 2
            P.dma("sp", QT[s], QTs[h], BQT, [BQTs[s]], "ldq%d" % s)
            P.dma("sp", KT[s], KTs[h], BKT, [BKTs[s]], "ldk%d" % s)
            P.dma("sp", Vh[s].rearrange("p (b d) -> p b d", b=64), Vs[:, :, h, :], BV, [BVh[s]], "ldv%d" % s)

        def postproc(h, j):
            B = [Bpp]
            par = (h * 8 + j) % 2

            def s0():
                ce = "act" if j <= 1 else "dve"
                P.copy(ce, O1sb, bank(4), [Bps[4]], [BO1sb])
                P.copy(ce, O2sb, bank(5), [Bps[5]], [BO2sb])
                P.copy("dve", Rhi, bank(6), [Bps[6]], [BRsb])
                P.tt("dve", Rlo, bank(6), Rhi, SUB, [Bps[6], BRsb], [BRsb])

            def s1():
                P.mm(bank(7), sel1b, Rhi, True, False, [BRsb, Bcp], [Bps[7]])
                P.mm(bank(7), sel1b, Rlo, False, True, [BRsb, Bcp], [Bps[7]])
                P.copy("dve", rl1, bank(7), [Bps[7]], B)

            def s2():
                P.mm(bank(7), sel2b, Rhi, True, False, [BRsb, Bcp], [Bps[7]])
                P.mm(bank(7), sel2b, Rlo, False, True, [BRsb, Bcp], [Bps[7]])
                P.tt("dve", osb, O1sb, bank(7), MUL, [BO1sb, Bps[7]] + B, B)
                P.tt("dve", rl2, rl1, bank(7), MUL, [Bps[7]] + B, B)
                P.tt("dve", tsb, O2sb, rl1, MUL, [BO2sb] + B, B)
                P.stt("dve", osb, tsb, neglam, osb, MUL, ADD, B + [Bc], B)
                P.tt("dve", tsb, osb, osb, MUL, B, B)
                P.copy("dve", qhi, tsb, B, B)
                P.tt("dve", qlo, tsb, qhi, SUB, B, B)
                P.stt("dve", rl2, rl2, RMS_EPS, rl2, MUL, MUL, B, B)

            def s3():
                P.mm(bank(7), ones_bf, qhi, True, False, B + [Bcp], [Bps[7]])
                P.mm(bank(7), ones_bf, qlo, False, True, B + [Bcp], [Bps[7]])
                P.stt("dve", asb, bank(7), 1.0 / 128, rl2, MUL, ADD, [Bps[7]] + B, B)
                rsqrt("dve", ysb, asb, tsb, B, 2)
                P.stt("dve", ao[par], osb, gsub, ysb, MUL, MUL, B + [Bc], [Bao[par]])
                P.dma("sp", ATs[:, h, j * TS:(j + 1) * TS], ao[par], [Bao[par]], [BATs[j]], "sta%d" % par)

            return s0, [(4, s1), (6, s2), (12, s3)]

        steps_all = []
        for h in range(8):
            for j in range(8):
                steps = []
                for kt in range(j):
                    steps += [(kt * 4 + i, 0, None, False) for i in range(4)]
                for kt in range(j):
                    steps += [(32 + kt * 4 + i, 0, None, False) for i in range(4)]
                steps += [(32 + j * 4 + i, 0, j, False) for i in range(4)]
                steps += [(j * 4 + i, 128 * i, None, True) for i in range(4)]
                for si, (kb, q0, vj, diag) in enumerate(steps):
                    steps_all.append((h, j, si, len(steps), kb, q0, vj, diag))

        def emit_qk(idx):
            h, j, si, nst, kb, q0, vj, diag = steps_all[idx]
            hs = h % 2
            sbase = (idx % 2) * 1024
            Bs = [Bps[2 * (idx % 2)], Bps[2 * (idx % 2) + 1]]
            qsl = slice(j * TS + q0, (j + 1) * TS)
            ksl = slice(kb * 128, (kb + 1) * 128)
            P.mm(psum[:, sbase + q0:sbase + 512], KT[hs][0:64, ksl], QT[hs][0:64, qsl], True, True, [BKTs[hs], BQTs[hs]], Bs)
            P.mm(psum[:, sbase + 512 + q0:sbase + 1024], KT[hs][64:128, ksl], QT[hs][64:128, qsl], True, True, [BKTs[hs], BQTs[hs]], Bs, tp=(64, 0))
            if diag:
                for m in range(2):
                    P.op("pe", lambda e, o_=psum[:, sbase + 512 * m + q0:sbase + 512 * m + q0 + 128]: e.matmul(o_, lhsT=ident, rhs=negm, start=False, stop=True, skip_group_check=True), [Bc, Bcp], Bs)

        pending = []

        def emit_rest(idx):
            h, j, si, nst, kb, q0, vj, diag = steps_all[idx]
            hs = h % 2
            pb = idx % NPT
            sbase = (idx % 2) * 1024
            Bs = [Bps[2 * (idx % 2)], Bps[2 * (idx % 2) + 1]]
            S3 = psum[:, sbase:sbase + 1024].rearrange("p (m n) -> p m n", m=2)
            PT3 = PT[pb].rearrange("p (m n) -> p m n", m=2)
            bias = 0.0 if vj is None else vis[:, vj:vj + 1]
            P.act(PT3[:, :, q0:512], S3[:, :, q0:512], AF.Exp, Bs + [Bc], [BPT[pb]], bias=bias, scale=0.125)
            first, last = si == 0, si == nst - 1
            vsl = slice(kb * 128, (kb + 1) * 128)
            if idx + 2 < NS:
                emit_qk(idx + 2)
            for m in range(2):
                P.mm(bank(4 + m)[:, q0:512], Vh[hs][:, vsl], PT3[:, m, q0:512], first, last, [BVh[hs], BPT[pb]], [Bps[4 + m]])
            if si % 2 == 1:
                for wh, sidx in ((0, idx - 1), (1, idx)):
                    _, _, si_, _, _, q0_, _, _ = steps_all[sidx]
                    PTx = PT[sidx % NPT].rearrange("p (m n) -> p m n", m=2)
                    for m in range(2):
                        ro = 64 * wh + 32 * m
                        P.mm(bank(6)[ro:ro + 32, q0_:512], ones_bf[:, 0:32], PTx[:, m, q0_:512], si_ < 2, si_ >= nst - 2, [Bcp, BPT[sidx % NPT]], [Bps[6]], tp=(0, ro))
            if last:
                while pending:
                    pending.pop(0)[1]()
                s0, rest = postproc(h, j)
                s0()
                pending.extend(rest)
                if j == 7 and h + 2 < 8:
                    load_head(h + 2)
            elif pending and si >= pending[0][0]:
                pending.pop(0)[1]()

        load_head(0)
        load_head(1)
        NS = len(steps_all)
        emit_qk(0)
        emit_qk(1)
        for i in range(NS):
            emit_rest(i)
        while pending:
            pending.pop(0)[1]()

        P.barrier()
        off[0] = m2

        NSLOT = 5
        wr = [alloc(8 * 1024, BF16) for _ in range(NSLOT)]; Bwr = [Buf("wr%d" % i) for i in range(NSLOT)]
        xt3s = [alloc(8 * TS, BF16) for _ in range(2)]; Bxt3s = [Buf("xt3_0"), Buf("xt3_1")]
        at3s = [alloc(8 * TS, BF16) for _ in range(2)]; Bat3s = [Buf("at3_0"), Buf("at3_1")]
        pTs = [alloc(2 * TS, BF16) for _ in range(2)]; BpTs = [Buf("pT0"), Buf("pT1")]
        xtok = [alloc(1024) for _ in range(2)]; Bxtok = [Buf("xtok0"), Buf("xtok1")]
        vg2 = [alloc(1024) for _ in range(2)]; Bvg2 = [Buf("vg0"), Buf("vg1")]; vg = vg2[0]; Bvg = Bvg2[0]
        vhat = alloc(4 * 1024, BF16); Bvhat = [Buf("vhat%d" % i) for i in range(4)]
        sa = alloc(8 * TS, BF16); Bsa = [Buf("sa%d" % g) for g in range(8)]
        U = alloc(8 * TS, BF16); BU = [Buf("U%d" % g) for g in range(8)]
        Zt = [alloc(TS, BF16) for _ in range(2)]; BZt = [Buf("Z0"), Buf("Z1")]
        G = [alloc(TS) for _ in range(2)]; BG = [Buf("G0"), Buf("G1")]
        mrg = alloc(8 * TS, BF16); Bmrg = [Buf("mrg%d" % d) for d in range(8)]
        sig = vg; Bsig = Bvg
        tt2 = [alloc(1024) for _ in range(2)]; Btt2 = [Buf("tt0"), Buf("tt1")]; tt_ = tt2[0]; Btt = Btt2[0]
        yy = [alloc(1024) for _ in range(2)]; Byy = [Buf("yy%d" % i) for i in range(2)]
        lngb = alloc(1024); lnbb = alloc(1024)
        Cg = alloc(8 * 128); wsTf = tt_; wsTb = alloc(8 * 128, BF16); bsb = yy[0]
        NST = 2
        stt_ = [dict(st6=alloc(12), mv=alloc(2), va=alloc(1), rs=alloc(1), tm=alloc(1), nmr=alloc(1), B=Buf("stats%d" % i)) for i in range(NST)]
        stn = [0]
        Bc3 = Buf("c3")

        P.dma("sp", lngb, lng_d.partition_broadcast(128), [], [Bc3], "c3")
        P.dma("sp", lnbb, lnb_d.partition_broadcast(128), [], [Bc3], "c3")
        P.dma("sp", wsTf.rearrange("p (g t) -> p g t", g=8), wsT_d.rearrange("g s t -> s g t"), [], [Bc3, Btt], "c3")
        P.dma("sp", bsb, bs_d.rearrange("g t -> (g t)").partition_broadcast(128), [], [Bc3, Byy[0]], "c3")
        P.tt("dve", wsTb.rearrange("p (g t) -> p g t", g=8), wsTf.rearrange("p (g t) -> p g t", g=8), trif.unsqueeze(1).broadcast_to([128, 8, 128]), MUL, [Bc3, Bc, Btt], [Bc3])
        for half in range(2):
            P.mm(bank(half), ones_bf, wsTb[:, half * 512:(half + 1) * 512], True, True, [Bc3, Bcp], [Bps[half]])
        for g in range(8):
            P.stt("dve", Cg[:, g * 128:(g + 1) * 128], psum[:, g * 128:(g + 1) * 128], alnb[:, g:g + 1], bsb[:, g * 128:(g + 1) * 128], MUL, ADD, [Bps[g // 4], Bc3, Bc, Byy[0]], [Bc3])

        nchunks = NTILE * len(CH)
        loaded = [0]

        def wload():
            n = loaded[0]
            if n >= nchunks:
                return
            loaded[0] += 1
            nm = CH[n % len(CH)]
            s = n % NSLOT
            if nm == "wp":
                P.dma("sp", wr[s][:, 0:2048].rearrange("p (k c) -> p k c", k=2), Wsc[nm].rearrange("(k p) c -> p k c", p=128), [BW[nm]], [Bwr[s]], "wr%d" % s)
            else:
                P.dma("sp", wr[s].rearrange("p (k c) -> p k c", k=8), Wsc[nm].rearrange("(k p) c -> p k c", p=128), [BW[nm]], [Bwr[s]], "wr%d" % s)

        used = [0]

        def wcur(expect, ahead=0):
            n = used[0] + ahead
            assert CH[n % len(CH)] == expect
            s = n % NSLOT
            return wr[s].rearrange("p (k c) -> p k c", k=8), Bwr[s]

        def wdone():
            used[0] += 1
            wload()

        rr1 = [0]
        rr2 = [0]

        def nb1():
            b = (4 + rr1[0]) % 8
            rr1[0] += 1
            return b

        def nb2(nset=2):
            b = 2 * (rr2[0] % nset)
            rr2[0] += 1
            return b

        def load_tile_inputs(j):
            s_ = j % 2
            tsl_ = slice(j * TS, (j + 1) * TS)
            P.dma("sp", xt3s[s_].rearrange("p (k n) -> p k n", k=8), xTb[:, :, tsl_], [BxTb[j]], [Bxt3s[s_]], "xt3_%d" % s_)
            P.dma("pool", pTs[s_].rearrange("p (k n) -> p k n", k=2), pT_d.rearrange("(k p) n -> p k n", p=128)[:, :, tsl_], [], [BpTs[s_]], "pT%d" % s_)
            P.dma("sp", at3s[s_].rearrange("p (h n) -> p h n", h=8), ATs[:, :, tsl_], [BATs[j]], [Bat3s[s_]], "at3_%d" % s_)

        def load_xtok(gb):
            xs_ = gb % 2
            P.dma("sp", xtok[xs_], x_own[gb * 128:(gb + 1) * 128, :], [], [Bxtok[xs_]], "xtok%d" % xs_)

        load_tile_inputs(0)
        for _ in range(NSLOT):
            wload()
        load_xtok(0)

        U3 = U.rearrange("p (g n) -> p g n", g=8)
        sa3 = sa.rearrange("p (g n) -> p g n", g=8)
        mrg3 = mrg.rearrange("p (g n) -> p g n", g=8)
        vhat3 = vhat.rearrange("p (b c) -> p b c", b=4)
        out_ops = []

        def layernorm(src, Bsrc, dst, Bdst):
            S_ = stt_[stn[0] % NST]
            stn[0] += 1
            B_ = [S_["B"]]
            st6, mv, va_, rs_, tm_, nmr = S_["st6"], S_["mv"], S_["va"], S_["rs"], S_["tm"], S_["nmr"]
            P.op("dve", lambda e: e.bn_stats(out=st6[:, 0:6], in_=src[:, 0:512]), [Bsrc], B_)
            P.op("dve", lambda e: e.bn_stats(out=st6[:, 6:12], in_=src[:, 512:1024]), [Bsrc] + B_, B_)
            P.op("dve", lambda e: e.bn_aggr(out=mv, in_=st6), B_, B_)
            P.ts("dve", va_, mv[:, 1:2], LN_EPS, None, ADD, None, B_, B_)
            rsqrt("dve", rs_, va_, tm_, B_, 2)
            P.stt("dve", nmr, mv[:, 0:1], -1.0, rs_, MUL, MUL, B_, B_)
            P.act(dst, src, AF.Identity, [Bsrc] + B_, [Bdst], bias=nmr, scale=rs_)

        def fm_proj(wname, dst_fn):
            w3, Bw = wcur(wname)
            for dc in range(8):
                bk = nb1()
                for kc in range(8):
                    P.mm(bank(bk), w3[:, kc, dc * 128:(dc + 1) * 128], x3[:, kc, :], kc == 0, kc == 7, [Bw, Bxt3], [Bps[bk]])
                dst_fn(dc, bk)
            wdone()

        def gated_proj(wname, gname, rhs_fn, Brhs_fn, combine):
            w3a, Bwa = wcur(wname)
            w3g, Bwg = wcur(gname, 1)
            for dc in range(8):
                bka = nb1()
                for kc in range(8):
                    P.mm(bank(bka), w3a[:, kc, dc * 128:(dc + 1) * 128], rhs_fn(kc), kc == 0, kc == 7, [Bwa, Brhs_fn(kc)], [Bps[bka]])
                bkg = nb1()
                for kc in range(8):
                    P.mm(bank(bkg), w3g[:, kc, dc * 128:(dc + 1) * 128], x3[:, kc, :], kc == 0, kc == 7, [Bwg, Bxt3], [Bps[bkg]])
                z = dc % 2
                P.act(G[z], bank(bkg), AF.Sigmoid, [Bps[bkg]], [BG[z]])
                combine(dc, bka, z)
            wdone()
            wdone()

        for j in range(NTILE):
            ts_ = j % 2
            x3 = xt3s[ts_].rearrange("p (k n) -> p k n", k=8); Bxt3 = Bxt3s[ts_]
            p3 = pTs[ts_].rearrange("p (k n) -> p k n", k=2); BpT = BpTs[ts_]
            attn3 = at3s[ts_].rearrange("p (h n) -> p h n", h=8); Bat3 = Bat3s[ts_]
            if j + 1 < NTILE:
                load_tile_inputs(j + 1)
            w3, Bw = wcur("va")
            for blk in range(4):
                b0 = nb2()
                for half in range(2):
                    for kc in range(8):
                        P.mm(bank(b0 + half), x3[:, kc, blk * 128:(blk + 1) * 128], w3[:, kc, half * 512:(half + 1) * 512], kc == 0, kc == 7, [Bw, Bxt3], [Bps[b0 + half]])
                P.act(vg2[blk % 2], bank2(b0), AF.Gelu, [Bps[b0], Bps[b0 + 1]], [Bvg2[blk % 2]])
                layernorm(vg2[blk % 2], Bvg2[blk % 2], vhat3[:, blk, :], Bvhat[blk])
            wdone()

            def ua_dst(dc, bk):
                P.act(U3[:, dc, :], bank(bk), AF.Gelu, [Bps[bk]], [BU[dc]])
            fm_proj("ua", ua_dst)
            for g in range(8):
                bk = nb1()
                for blk in range(4):
                    P.mm(bank(bk)[:, blk * 128:(blk + 1) * 128], vhat3[:, blk, g * 128:(g + 1) * 128], wsTb[:, g * 128:(g + 1) * 128], True, True, [Bvhat[blk], Bc3], [Bps[bk]])
                P.stt("dve", sa3[:, g, :].rearrange("p (b t) -> p b t", b=4), bank(bk).rearrange("p (b t) -> p b t", b=4), alng[:, g:g + 1],
                      Cg[:, g * 128:(g + 1) * 128].unsqueeze(1).broadcast_to([128, 4, 128]), MUL, ADD, [Bps[bk], Bc3, Bc], [Bsa[g]])
            for dc in range(8):
                P.tt("pool", U3[:, dc, :], U3[:, dc, :], sa3[:, dc, :], MUL, [BU[dc], Bsa[dc]], [BU[dc]])

            def za_dst(dc, bk):
                z = dc % 2
                P.act(Zt[z], bank(bk), AF.Silu, [Bps[bk]], [BZt[z]])
                P.tt("pool", U3[:, dc, :], U3[:, dc, :], Zt[z], MUL, [BU[dc], BZt[z]], [BU[dc]])
            fm_proj("za", za_dst)

            def zb_dst(dc, bk):
                z = dc % 2
                P.act(Zt[z], bank(bk), AF.Silu, [Bps[bk]], [BZt[z]])
                P.tt("pool", attn3[:, dc, :], attn3[:, dc, :], Zt[z], MUL, [Bat3, BZt[z]], [Bat3])
            fm_proj("zb", zb_dst)

            def comb_a(dc, bka, z):
                P.tt("dve", mrg3[:, dc, :], bank(bka), G[z], MUL, [Bps[bka], BG[z]], [Bmrg[dc]])
            gated_proj("wa", "ga", lambda kc: U3[:, kc, :], lambda kc: BU[kc], comb_a)

            def comb_b(dc, bkb, z):
                P.tt("dve", G[z], bank(bkb), G[z], MUL, [Bps[bkb], BG[z]], [BG[z]])
                P.tt("pool", mrg3[:, dc, :], mrg3[:, dc, :], G[z], ADD, [Bmrg[dc], BG[z]], [Bmrg[dc]])
            gated_proj("wb", "gb", lambda kc: attn3[:, kc, :], lambda kc: Bat3, comb_b)

            w3o, Bwo = wcur("wout")
            w3pg, Bwpg = wcur("wpg", 1)
            w3p, Bwp = wcur("wp", 2)
            for blk in range(4):
                gb = j * 4 + blk
                r0 = gb * 128
                bsl = slice(blk * 128, (blk + 1) * 128)
                xs = gb % 2
                ys = gb % 2
                tt_ = tt2[gb % 2]; Btt = Btt2[gb % 2]
                if gb + 1 < 32:
                    load_xtok(gb + 1)
                bpg = nb2(4)
                for half in range(2):
                    for kc in range(8):
                        P.mm(bank(bpg + half), x3[:, kc, bsl], w3pg[:, kc, half * 512:(half + 1) * 512], kc == 0, kc == 7, [Bwpg, Bxt3], [Bps[bpg + half]])
                P.act(sig, bank2(bpg), AF.Sigmoid, [Bps[bpg], Bps[bpg + 1]], [Bsig])
                bpw = nb2(4)
                for half in range(2):
                    for kc in range(2):
                        P.mm(bank(bpw + half), p3[:, kc, bsl], w3p[:, kc, half * 512:(half + 1) * 512], kc == 0, kc == 1, [Bwp, BpT], [Bps[bpw + half]])
                P.tt("dve", tt_, bank2(bpw), sig, MUL, [Bps[bpw], Bps[bpw + 1], Bsig], [Btt])
                P.stt("dve", tt_, xtok[xs], ALPHA, tt_, MUL, ADD, [Bxtok[xs], Btt], [Btt])
                bmx = nb2(4)
                for half in range(2):
                    for kc in range(8):
                        P.mm(bank(bmx + half), mrg3[:, kc, bsl], w3o[:, kc, half * 512:(half + 1) * 512], kc == 0, kc == 7, [Bwo, Bmrg[kc]], [Bps[bmx + half]])
                P.tt("dve", tt_, bank2(bmx), tt_, ADD, [Bps[bmx], Bps[bmx + 1], Btt], [Btt])
                layernorm(tt_, Btt, yy[ys], Byy[ys])
                P.tt("pool", yy[ys], yy[ys], lngb, MUL, [Byy[ys], Bc3], [Byy[ys]])
                P.tt("pool", yy[ys], yy[ys], lnbb, ADD, [Byy[ys], Bc3], [Byy[ys]])
                out_ops.append(P.dma("sp", out_d[r0:r0 + 128, :], yy[ys], [Byy[ys]], [], "ost%d" % ys))
            wdone()
            wdone()
            wdone()

        P.emit(nc, final_waits=out_ops)
    return nc


_NC_CACHE = {}


def _core_layout(core):
    b, r = core // 2, core % 2
    own = [2 * j + ((j & 1) ^ r) for j in range(8)]
    oth = [2 * j + 1 - ((j & 1) ^ r) for j in range(8)]
    tok_own = np.concatenate([np.arange(g * TS, (g + 1) * TS) for g in own])
    tok_oth = np.concatenate([np.arange(g * TS, (g + 1) * TS) for g in oth])
    vis = np.array([0.0 if ((j & 1) ^ r) == 1 else NEG for j in range(8)], dtype=np.float32)
    return b, tok_own, tok_oth, vis


def kernel(x, p, positions, w_in, a_ln_g, a_ln_b, a_w_s, a_b_s, b_lam_q1, b_lam_k1, b_lam_q2, b_lam_k2,
           b_subln_g, w_branch_a, w_branch_b, w_out, w_ple, w_ple_gate, ln_g, ln_b):
    f = lambda a: np.ascontiguousarray(np.asarray(a))
    x = f(x); p = f(p); positions = f(positions)
    if "nc" not in _NC_CACHE:
        _NC_CACHE["nc"] = build_program()
    nc = _NC_CACHE["nc"]
    tri = np.triu(np.ones((128, 128), dtype=np.float32))
    invf = (np.float32(10000.0) ** (-(np.arange(0, 64, 2, dtype=np.float32)) / np.float32(64))).astype(np.float32)
    shared = {
        "w_in": f(w_in[0]), "w_a": f(w_branch_a[0]), "w_b": f(w_branch_b[0]), "w_out": f(w_out[0]),
        "w_pg": f(w_ple_gate[0]), "w_p": f(w_ple[0]),
        "wsT": f(np.transpose(np.asarray(a_w_s[0]), (0, 2, 1))),
        "tri": tri, "bs": f(a_b_s[0]),
        "alng": f(np.asarray(a_ln_g[0]).reshape(8, 128).T), "alnb": f(np.asarray(a_ln_b[0]).reshape(8, 128).T),
        "subg": f(np.asarray(b_subln_g[0]).reshape(128, 1)),
        "lamv": f(np.stack([np.asarray(b_lam_q1[0]), np.asarray(b_lam_k1[0]), np.asarray(b_lam_q2[0]), np.asarray(b_lam_k2[0])])),
        "lng": f(ln_g[0]), "lnb": f(ln_b[0]),
        "invf": f(np.broadcast_to(invf[None, :], (128, 32))),
    }
    in_maps = []
    layouts = []
    for c in range(NCORES):
        b, tok_own, tok_oth, vis = _core_layout(c)
        tok_all = np.concatenate([tok_own, tok_oth])
        m = dict(shared)
        m["xT_all"] = f(x[b][tok_all].T)
        m["x_own"] = f(x[b][tok_own])
        m["pT"] = f(p[0, b][tok_own].T)
        m["pos"] = f(positions[b][tok_all].reshape(64, 128).T.astype(np.int32))
        m["vis"] = f(np.broadcast_to(vis[None, :], (128, 8)))
        in_maps.append(m)
        layouts.append((b, tok_own))
    res = run_bass_kernel_spmd(nc, in_maps, core_ids=list(range(NCORES)))
    out = np.empty((4, SEQ, D), dtype=np.float32)
    for c in range(NCORES):
        b, tok_own = layouts[c]
        out[b, tok_own] = res.results[c]["out"]
    return out
```
